# Optimizing a Trainium2 kernel written in Bass

```python
import math
import jax, jax.numpy as jnp
from jax import lax
import numpy as np

D_MODEL = 2048
BATCH = 8
SEQ = 2048
DEPTH = 1

GRID_W = 64
CTX_LEN = 256
HEAD_DIM = 128
N_HEADS = 8
N_KV_HEADS = 2
Q_PER_KV = N_HEADS // N_KV_HEADS
ATTN_W = N_HEADS * HEAD_DIM
KV_W = N_KV_HEADS * HEAD_DIM
POOL_WINDOWS = (2, 4, 8, 16)
N_POOL_GROUPS = len(POOL_WINDOWS)
POOL_W = D_MODEL // 2
POOL_GROUP_W = POOL_W // N_POOL_GROUPS
MIX_W = POOL_W + ATTN_W
PROJ_W = POOL_W + ATTN_W + 2 * KV_W
D_FF = ((8 * D_MODEL // 3 + 255) // 256) * 256
ROPE_THETA = 10000.0
AXIS_ROT = HEAD_DIM // 2
Q_BLOCK = 128
EPS = 1e-6
N_MOD = 6

kernel_name = "hybrid_pool_gqa_dit_block"


def _rmsnorm(x, g):
    xf = x.astype(jnp.float32)
    y = xf * lax.rsqrt(jnp.mean(xf * xf, axis=-1, keepdims=True) + EPS)
    return (y * g.astype(jnp.float32)).astype(x.dtype)


def _split_proj(p):
    pool = p[..., :POOL_W]
    q = p[..., POOL_W:POOL_W + ATTN_W]
    k = p[..., POOL_W + ATTN_W:POOL_W + ATTN_W + KV_W]
    v = p[..., POOL_W + ATTN_W + KV_W:]
    return pool, q, k, v


def _heads_q(q):
    B, T, _ = q.shape
    return q.reshape(B, T, N_KV_HEADS, Q_PER_KV, HEAD_DIM)


def _heads_kv(k):
    B, T, _ = k.shape
    return k.reshape(B, T, N_KV_HEADS, HEAD_DIM)


def _axial_angles(T):
    n_rows = T // GRID_W
    rows = jnp.repeat(jnp.arange(n_rows, dtype=jnp.float32), GRID_W)
    cols = jnp.tile(jnp.arange(GRID_W, dtype=jnp.float32), n_rows)
    freqs = ROPE_THETA ** (-jnp.arange(0, AXIS_ROT, 2, dtype=jnp.float32) / AXIS_ROT)
    ang = jnp.concatenate([rows[:, None] * freqs, cols[:, None] * freqs], axis=-1)
    return jnp.cos(ang), jnp.sin(ang)


def _rope_2d(x, cos, sin):
    extra = x.ndim - 3
    cos = cos.reshape(cos.shape[0], *([1] * extra), cos.shape[-1])
    sin = sin.reshape(sin.shape[0], *([1] * extra), sin.shape[-1])
    xf = x.astype(jnp.float32).reshape(*x.shape[:-1], HEAD_DIM // 2, 2)
    x1, x2 = xf[..., 0], xf[..., 1]
    out = jnp.stack([x1 * cos - x2 * sin, x1 * sin + x2 * cos], axis=-1)
    return out.reshape(x.shape).astype(x.dtype)


def _attend(q, k, v):
    B, T = q.shape[0], q.shape[1]
    nb = T // Q_BLOCK
    scale = 1.0 / math.sqrt(HEAD_DIM)
    qb = q.reshape(B, nb, Q_BLOCK, N_KV_HEADS, Q_PER_KV, HEAD_DIM).transpose(1, 0, 2, 3, 4, 5)

    def one_block(q_blk):
        s = jnp.einsum('bqkgd,bskd->bkgqs', q_blk, k,
                       preferred_element_type=jnp.float32) * scale
        p = jax.nn.softmax(s, axis=-1).astype(v.dtype)
        return jnp.einsum('bkgqs,bskd->bqkgd', p, v)

    o = lax.map(one_block, qb)
    return o.transpose(1, 0, 2, 3, 4, 5).reshape(B, T, ATTN_W)


def _pool_mixer(u, w_grp, scale):
    B, T, _ = u.shape
    uf = u.astype(jnp.float32).reshape(B, T, N_POOL_GROUPS, POOL_GROUP_W)
    cs = jnp.concatenate([jnp.zeros_like(uf[:, :1]), jnp.cumsum(uf, axis=1)], axis=1)
    t = jnp.arange(T)[:, None]
    half = jnp.array(POOL_WINDOWS, dtype=jnp.int32)[None, :] // 2
    lo = jnp.clip(t - half, 0, T)
    hi = jnp.clip(t + half, 0, T)
    gi = jnp.arange(N_POOL_GROUPS)[None, :]
    win_mean = (cs[:, hi, gi] - cs[:, lo, gi]) / (hi - lo).astype(jnp.float32)[None, :, :, None]
    pooled = (win_mean - uf).astype(u.dtype)
    mixed = jnp.einsum('btgc,gcd->btgd', pooled, w_grp).reshape(B, T, POOL_W)
    return mixed * scale


def _swiglu(h, w_gate, w_up, w_down):
    return (jax.nn.silu(h @ w_gate) * (h @ w_up)) @ w_down


def setup_inputs(seed: int = 0) -> dict:
    key = jax.random.key(seed)
    ks = jax.random.split(key, 18)
    f32 = jnp.float32

    def nrm(k, shape, fan_in, mult=1.0):
        return jax.random.normal(k, shape, f32) * (mult * fan_in ** -0.5)

    return {
        "x": jax.random.normal(ks[0], (BATCH, SEQ, D_MODEL), f32),
        "c": jax.random.normal(ks[1], (BATCH, D_MODEL), f32),
        "ctx": jax.random.normal(ks[2], (BATCH, CTX_LEN, D_MODEL), f32),
        "c_ctx": jax.random.normal(ks[3], (D_MODEL,), f32),
        "w_ada": nrm(ks[4], (DEPTH, D_MODEL, N_MOD * D_MODEL), D_MODEL, 0.5),
        "b_ada": 0.02 * jax.random.normal(ks[5], (DEPTH, N_MOD * D_MODEL), f32),
        "norm_mix": 1.0 + 0.05 * jax.random.normal(ks[6], (DEPTH, D_MODEL), f32),
        "norm_ffn": 1.0 + 0.05 * jax.random.normal(ks[7], (DEPTH, D_MODEL), f32),
        "w_in": nrm(ks[8], (DEPTH, D_MODEL, PROJ_W), D_MODEL),
        "pool_w": nrm(ks[9], (DEPTH, N_POOL_GROUPS, POOL_GROUP_W, POOL_GROUP_W), POOL_GROUP_W),
        "pool_scale": 1.0 + 0.1 * jax.random.normal(ks[10], (DEPTH, POOL_W), f32),
        "q_norm": 1.0 + 0.05 * jax.random.normal(ks[11], (DEPTH, HEAD_DIM), f32),
        "k_norm": 1.0 + 0.05 * jax.random.normal(ks[12], (DEPTH, HEAD_DIM), f32),
        "w_out": nrm(ks[13], (DEPTH, MIX_W, D_MODEL), MIX_W),
        "w_gate": nrm(ks[14], (DEPTH, D_MODEL, D_FF), D_MODEL),
        "w_up": nrm(ks[15], (DEPTH, D_MODEL, D_FF), D_MODEL),
        "w_down": nrm(ks[16], (DEPTH, D_FF, D_MODEL), D_FF),
        "final_norm": 1.0 + 0.05 * jax.random.normal(ks[17], (D_MODEL,), f32),
    }


def reference(x, c, ctx, c_ctx, w_ada, b_ada, norm_mix, norm_ffn, w_in, pool_w,
              pool_scale, q_norm, k_norm, w_out, w_gate, w_up, w_down, final_norm):
    T = x.shape[1]
    cos, sin = _axial_angles(T)

    for layer in range(DEPTH):
        mod_lat = jax.nn.silu(c) @ w_ada[layer] + b_ada[layer]
        mod_ctx = jax.nn.silu(c_ctx) @ w_ada[layer] + b_ada[layer]
        sh_m, sc_m, g_m, sh_f, sc_f, g_f = jnp.split(mod_lat, N_MOD, axis=-1)
        csh_m, csc_m, cg_m, csh_f, csc_f, cg_f = jnp.split(mod_ctx, N_MOD, axis=-1)
        last = layer == DEPTH - 1

        hc = _rmsnorm(ctx, norm_mix[layer]) * (1.0 + csc_m) + csh_m
        if last:
            kv_c = hc @ w_in[layer][:, POOL_W + ATTN_W:]
            k_c, v_c = kv_c[..., :KV_W], kv_c[..., KV_W:]
        else:
            p_c, q_c, k_c, v_c = _split_proj(hc @ w_in[layer])
        k_c = _rmsnorm(_heads_kv(k_c), k_norm[layer])
        v_c = _heads_kv(v_c)

        hx = _rmsnorm(x, norm_mix[layer]) * (1.0 + sc_m[:, None]) + sh_m[:, None]
        p_x, q_x, k_x, v_x = _split_proj(hx @ w_in[layer])
        q_x = _rope_2d(_rmsnorm(_heads_q(q_x), q_norm[layer]), cos, sin)
        k_x = _rope_2d(_rmsnorm(_heads_kv(k_x), k_norm[layer]), cos, sin)
        k_all = jnp.concatenate([k_c, k_x], axis=1)
        v_all = jnp.concatenate([v_c, _heads_kv(v_x)], axis=1)
        attn_x = _attend(q_x, k_all, v_all)
        pool_x = _pool_mixer(p_x, pool_w[layer], pool_scale[layer])
        mix_x = jnp.concatenate([pool_x, attn_x], axis=-1) @ w_out[layer]

        if not last:
            q_c = _rmsnorm(_heads_q(q_c), q_norm[layer])
            attn_c = _attend(q_c, k_c, v_c)
            pool_c = _pool_mixer(p_c, pool_w[layer], pool_scale[layer])
            ctx = ctx + cg_m * (jnp.concatenate([pool_c, attn_c], axis=-1) @ w_out[layer])
            hfc = _rmsnorm(ctx, norm_ffn[layer]) * (1.0 + csc_f) + csh_f
            ctx = ctx + cg_f * _swiglu(hfc, w_gate[layer], w_up[layer], w_down[layer])

        x = x + g_m[:, None] * mix_x
        hf = _rmsnorm(x, norm_ffn[layer]) * (1.0 + sc_f[:, None]) + sh_f[:, None]
        x = x + g_f[:, None] * _swiglu(hf, w_gate[layer], w_up[layer], w_down[layer])

    return _rmsnorm(x, final_norm)
```

```python
import contextlib
import math

import numpy as np

import concourse.bass as bass
import concourse.mybir as mybir
from concourse.bass_utils import run_bass_kernel_spmd

F32 = mybir.dt.float32
BF16 = mybir.dt.bfloat16
AF = mybir.ActivationFunctionType
ALU = mybir.AluOpType
AX = mybir.AxisListType

T = 2048
D = 2048
NT = 16
CTX = 256
DFF = 5632
NF = 44
EPS = 1e-6
POOL_WINDOWS = (2, 4, 8, 16)
ENGS = ("pe", "act", "dve", "pool", "sp")

DEBUG_STOP = None


class Sched:
    def __init__(self):
        self.prog = {e: [] for e in ENGS}
        self.cnt = {}
        self.lastw = {}
        self.readers = {}
        self.seen = {e: {} for e in ENGS}
        self.bg_groups = set()

    def _collect(self, reads, writes):
        toks = {}

        def add(k, v):
            if toks.get(k, 0) < v:
                toks[k] = v

        for r in reads:
            t = self.lastw.get(r)
            if t is not None:
                add(*t)
        for w in writes:
            t = self.lastw.get(w)
            if t is not None:
                add(*t)
            for k, v in self.readers.get(w, {}).items():
                add(k, v)
        return toks

    def _emit_waits(self, eng, toks, skip_self):
        for k in sorted(toks):
            v = toks[k]
            if k == eng and skip_self:
                continue
            if self.seen[eng].get(k, 0) >= v:
                continue
            self.seen[eng][k] = v
            self.prog[eng].append(("wait", k, v))

    def _record(self, tok, reads, writes):
        k, v = tok
        for r in reads:
            d = self.readers.setdefault(r, {})
            if d.get(k, 0) < v:
                d[k] = v
        for w in writes:
            self.lastw[w] = tok
            self.readers[w] = {}

    def op(self, eng, fns, reads=(), writes=(), wars=()):
        if callable(fns):
            fns = [fns]
        toks = self._collect(reads, tuple(writes) + tuple(wars))
        self._emit_waits(eng, toks, skip_self=(eng == "pe"))
        self.cnt[eng] = self.cnt.get(eng, 0) + 1
        tok = (eng, self.cnt[eng])
        self.prog[eng].append(("op", fns, eng, 1))
        self._record(tok, reads, writes)
        return tok

    def dma(self, queue, out, in_, group, reads=(), writes=(), background=False):
        if group not in self.cnt:
            self.cnt[group] = 0
            if background:
                self.bg_groups.add(group)
        toks = self._collect(reads, writes)
        if self.cnt[group] > 0 and toks.get(group, 0) < self.cnt[group]:
            toks[group] = self.cnt[group]
        self._emit_waits(queue, toks, skip_self=False)
        self.cnt[group] += 16
        tok = (group, self.cnt[group])
        self.prog[queue].append(("op", [lambda e, o=out, i=in_: e.dma_start(out=o, in_=i)], group, 16))
        self._record(tok, reads, writes)
        return tok

    def barrier(self):
        toks = {k: v for k, v in self.cnt.items() if v > 0 and k not in self.bg_groups}
        for e in ENGS:
            self._emit_waits(e, dict(toks), skip_self=(e == "pe"))
        self.lastw = {r: t for r, t in self.lastw.items() if t[0] in self.bg_groups}
        self.readers = {}

    def final_wait(self, eng, groups):
        toks = {g: self.cnt[g] for g in groups if self.cnt.get(g, 0) > 0}
        self._emit_waits(eng, toks, skip_self=False)

    def generate(self, nc):
        keys = [k for k, v in self.cnt.items() if v > 0]
        with contextlib.ExitStack() as st:
            sems = {k: st.enter_context(nc.semaphore("s_" + str(k))) for k in keys}
            block = st.enter_context(nc.Block())

            def run(engname):
                def body(e):
                    for item in self.prog[engname]:
                        if item[0] == "wait":
                            e.wait_ge(sems[item[1]], item[2])
                        else:
                            _, fns, semk, inc = item
                            ins = None
                            for f in fns:
                                ins = f(e)
                            ins.then_inc(sems[semk], inc)
                return body

            block.tensor(run("pe"))
            block.scalar(run("act"))
            block.vector(run("dve"))
            block.gpsimd(run("pool"))
            block.sync(run("sp"))


def build_program(debug_stop=None):
    nc = bass.Bass("TRN2", target_bir_lowering=False)

    def din(name, shape):
        return nc.dram_tensor(name, list(shape), F32, kind="ExternalInput").ap()

    x_d = din("x", [T, D])
    ctx_d = din("ctx", [CTX, D])
    cc_d = din("cc", [128, 2, 16])
    wada_d = din("wada", [24, 128, 17, 512])
    vecs_d = din("vecs", [128, 40])
    gq_d = din("gq", [128])
    gk_d = din("gk", [128])
    fn_d = din("fn", [D])
    win_d = din("win", [5, 128, 16, 512])
    pw_d = din("pw", [128, 4, 2, 256])
    wo_d = din("wo", [8, 128, 4096])
    wgu_d = din("wgu", [NF, 128, 4096])
    wd_d = din("wd", [24, 128, 4096])
    cmat_d = din("cmat", [128, 22, 128])
    rope_d = din("rope", [128, 2, 16, 64])
    pfix_d = din("pfix", [128, 4, 2, 8])
    out_d = nc.dram_tensor("out", [T, D], F32, kind="ExternalOutput").ap()
    wo_s = nc.dram_tensor("wo_s", [8, 128, 4096], BF16, kind="Internal").ap()
    wgu_s = nc.dram_tensor("wgu_s", [NF, 128, 4096], BF16, kind="Internal").ap()
    wd_s = nc.dram_tensor("wd_s", [24, 128, 4096], BF16, kind="Internal").ap()
    dbg = {}
    if debug_stop is not None:
        dbg["qT"] = nc.dram_tensor("dbg_qT", [128, 8, T], BF16, kind="ExternalOutput").ap()
        dbg["kT"] = nc.dram_tensor("dbg_kT", [128, 2, T + CTX], BF16, kind="ExternalOutput").ap()
        dbg["V"] = nc.dram_tensor("dbg_V", [128, 18, 256], BF16, kind="ExternalOutput").ap()
        dbg["U"] = nc.dram_tensor("dbg_U", [128, 16, 1024], BF16, kind="ExternalOutput").ap()
        dbg["CP"] = nc.dram_tensor("dbg_CP", [128, 8, T], BF16, kind="ExternalOutput").ap()
        dbg["GB"] = nc.dram_tensor("dbg_GB", [128, 3, D], F32, kind="ExternalOutput").ap()
        dbg["mod"] = nc.dram_tensor("dbg_mod", [128, 64, 2], F32, kind="ExternalOutput").ap()

    s = Sched()
    with contextlib.ExitStack() as st:
        ARENA_W = 53200
        arena = st.enter_context(nc.sbuf_tensor("arena", [128, ARENA_W], F32))
        P = [st.enter_context(nc.psum_tensor("ps%d" % i, [128, 1024], F32)) for i in range(4)]

        def view(off, shape, dt):
            esz = 4 if dt == F32 else 2
            n = 1
            for d_ in shape[1:]:
                n *= d_
            nb = n * esz
            assert off % 4 == 0 and nb % 4 == 0 and off + nb <= ARENA_W * 4, (off, shape)
            a = arena[:, off // 4:(off + nb) // 4]
            if dt != F32:
                a = a.bitcast(dt)
            if len(shape) == 3:
                a = a.rearrange("p (a b) -> p a b", a=shape[1])
            elif len(shape) == 4:
                a = a.rearrange("p (a b c) -> p a b c", a=shape[1], b=shape[2])
            return a

        def bank(b):
            return P[b // 2][:, (b % 2) * 512:(b % 2) * 512 + 512]

        def bank_bf(b, n):
            return bank(b).bitcast(BF16)[:, 0:n * 128].rearrange("p (a b) -> p a b", a=n)

        O_QT = 0
        O_CP = 32768
        O_GB = 65536
        O_SM = 90112
        O_OV = 93696
        QT = view(O_QT, [128, 8, T], BF16)
        CP = view(O_CP, [128, 8, T], BF16)
        GB = view(O_GB, [128, 3, D], F32)
        o = O_SM
        ident = view(o, [128, 128], BF16); o += 256
        ones = view(o, [128, 128], BF16); o += 256
        vecs = view(o, [128, 40], F32); o += 160
        ABv = view(o, [128, 3, 16], F32); o += 192
        modsb = view(o, [128, 64, 2], F32); o += 512
        gqk = view(o, [128, 2, 128], F32); o += 1024
        sc2 = view(o, [128, 16, 2], BF16); o += 64
        cc = view(o, [128, 2, 16], F32); o += 128
        stat = view(o, [128, 64], F32); o += 256
        pfix = view(o, [128, 4, 2, 8], F32); o += 256
        assert o <= O_OV

        def cat_chunk(j):
            return CP[:, j, :] if j < 8 else QT[:, j - 8, :]

        o = O_OV
        U = view(o, [128, 16, 1024], BF16); o += 32768
        kT = view(o, [128, 2, T + CTX], BF16); o += 9216
        V = view(o, [128, 18, 256], BF16); o += 9216
        bands = view(o, [128, 20, 128], BF16); o += 5120
        scbc = view(o, [128, 16, 128], BF16); o += 4096
        O_OV2 = o
        wring = [view(o + i * 17408, [128, 17, 512], BF16) for i in range(2)]; o += 34816
        xt = [view(o + i * 8192, [128, D], F32) for i in range(2)]; o += 16384
        assert o <= ARENA_W * 4
        hxB = [view(O_CP + i * 16384, [128, 16, 512], BF16) for i in range(2)]
        o = O_GB
        xn = [view(o + i * 4096, [128, D], BF16) for i in range(2)]; o += 8192
        QTMP = 5120

        def mk_qtmp(b0):
            return dict(qr=view(b0, [128, 512], BF16), t1=view(b0 + 1024, [128, 4, 64], F32), junk=view(b0 + 1024, [128, 512], F32),
                        t2=view(b0 + 2048, [128, 4, 64], F32), qf=view(b0 + 3072, [128, 512], F32))

        qtmp = [mk_qtmp(o), mk_qtmp(O_OV2 + 34816 + 16384)]
        assert O_OV2 + 34816 + 16384 + QTMP <= ARENA_W * 4
        o += QTMP
        rope = view(o, [128, 2, 16, 64], F32); o += 8192
        assert o <= O_SM
        o = O_OV2
        E = [view(o + i * 2048, [128, 1024], BF16) for i in range(3)]; o += 6144
        rD = view(o, [128, 512], F32); o += 2048
        pooled = [view(o + i * 2048, [128, 2, 512], BF16) for i in range(2)]; o += 4096
        poolw = view(o, [128, 4, 2, 256], BF16); o += 4096
        wring2 = [view(o + i * 17408, [128, 17, 512], BF16) for i in range(2)]; o += 34816
        Osb = view(o, [128, 512], F32); o += 2048
        Dsb = view(o, [128, 512], F32); o += 2048
        assert o <= ARENA_W * 4
        o = O_OV
        x1 = view(o, [128, 4, D], F32); o += 32768
        actT = view(o, [128, NF, 512], BF16); o += 45056
        ring = [view(o + i * 8192, [128, 4096], BF16) for i in range(3)]; o += 24576
        xn3 = [view(o + i * 4096, [128, D], BF16) for i in range(2)]; o += 8192
        sg = [view(o + i * 2048, [128, 512], F32) for i in range(2)]; o += 4096
        tmp3 = [view(o + i * 2048, [128, 512], F32) for i in range(2)]; o += 4096
        assert o <= ARENA_W * 4, o

        s.dma("sp", cc, cc_d, "g_sm", writes=["cc"])
        s.dma("sp", vecs, vecs_d, "g_sm", writes=["vecs"])
        s.dma("sp", gqk[:, 0, :], gq_d.partition_broadcast(128), "g_sm", writes=["gq"])
        s.dma("sp", gqk[:, 1, :], gk_d.partition_broadcast(128), "g_sm", writes=["gk"])
        s.dma("sp", rope, rope_d, "g_sm", writes=["rope"])
        s.dma("sp", pfix, pfix_d, "g_sm", writes=["pfix"])
        s.dma("pool", ident, cmat_d[:, 0, :], "g_c", writes=["ident"])
        s.dma("pool", ones, cmat_d[:, 1, :], "g_c", writes=["ones"])
        s.dma("pool", bands, cmat_d[:, 2:22, :], "g_c", writes=["bands"])

        s.op("act", lambda e: e.activation(out=sc2[:, :, 0], in_=cc[:, 0, :], func=AF.Silu), reads=["cc"], writes=["sc2a"])
        s.op("act", lambda e: e.activation(out=sc2[:, :, 1], in_=cc[:, 1, :], func=AF.Silu), reads=["cc"], writes=["sc2b"])
        s.op("dve", lambda e: e.tensor_copy(out=scbc, in_=sc2[:, :, 0:1].to_broadcast([128, 16, 128])), reads=["sc2a"], writes=["scbc"])
        s.op("dve", lambda e: e.tensor_scalar(out=gqk[:, 0, :], in0=gqk[:, 0, :], scalar1=1.0 / math.sqrt(128.0), scalar2=0.0, op0=ALU.mult, op1=ALU.add),
             reads=["gq"], writes=["gq"])

        pc_list = []
        for kind, src, dst, n in (("wo", wo_d, wo_s, 8), ("wgu", wgu_d, wgu_s, NF), ("wd", wd_d, wd_s, 24)):
            for c in range(n // 4):
                pc_list.append((kind, c, src, dst))
        pc_state = {"i": 0}

        def precast(k=1):
            for _ in range(k):
                i = pc_state["i"]
                if i >= len(pc_list):
                    return
                kind, c, src, dst = pc_list[i]
                s.dma("pool", dst[4 * c:4 * c + 4].rearrange("a p f -> (a p) f"), src[4 * c:4 * c + 4].rearrange("a p f -> (a p) f"),
                      "g_pc%d" % (i % 3), writes=[("scr", kind, c)], background=True)
                pc_state["i"] += 1

        MODBANK = 6
        GBANK = 7
        modps = bank(MODBANK)[:, 0:128].rearrange("p (a b) -> p a b", b=2)

        def wada_chunk(ch, slot, slotkey):
            part = ch // 4
            if part in (2, 5):
                gi = 0 if part == 2 else 1
                cols = slice((ch % 4) * 512, (ch % 4) * 512 + 512)
                fns = [lambda e, kc=kc: e.matmul(bank(GBANK), lhsT=scbc[:, kc, :], rhs=slot[:, kc, :], start=(kc == 0), stop=False) for kc in range(16)]
                fns.append(lambda e: e.matmul(bank(GBANK), lhsT=ones[0:1, :], rhs=slot[0:1, 16, :], start=False, stop=True))
                s.op("pe", fns, reads=[slotkey, "scbc", "ones"], writes=[("ps", GBANK)])
                s.op("act", lambda e: e.activation(out=GB[:, gi, cols], in_=bank(GBANK), func=AF.Copy), reads=[("ps", GBANK)], writes=[("GB", gi, ch % 4)])
            else:
                base = {0: 0, 1: 16, 3: 32, 4: 48}[part] + (ch % 4) * 4
                fns = []
                for blk in range(4):
                    dst = modps[:, base + blk, :]
                    for kc in range(16):
                        fns.append(lambda e, kc=kc, blk=blk, dst=dst: e.matmul(dst, lhsT=slot[:, kc, blk * 128:(blk + 1) * 128], rhs=sc2[:, kc, :],
                                                                               start=(kc == 0), stop=False))
                    fns.append(lambda e, blk=blk, dst=dst: e.matmul(dst, lhsT=slot[0:1, 16, blk * 128:(blk + 1) * 128], rhs=ones[0:1, 0:2], start=False, stop=True))
                s.op("pe", fns, reads=[slotkey, "sc2a", "sc2b", "ones"], writes=[("ps", MODBANK)])
                s.op("dve", lambda e: e.tensor_copy(out=modsb[:, base:base + 4, :], in_=modps[:, base:base + 4, :]), reads=[("ps", MODBANK)], writes=["modsb"])

        ring1_items = [("wada", ch) for ch in range(8)] + [("win", 4)] + [("win", n) for _ in range(4) for n in range(5)]
        r1 = {"next": 0}

        def ring1_load():
            i = r1["next"]
            if i >= len(ring1_items):
                return
            kind, idx = ring1_items[i]
            slot = wring[i % 2]
            if kind == "wada":
                s.dma("pool", slot, wada_d[idx], "g_r1_%d" % (i % 2), writes=[("wr", i % 2)])
            else:
                s.dma("pool", slot[:, 0:16, :], win_d[idx], "g_r1_%d" % (i % 2), writes=[("wr", i % 2)])
            r1["next"] += 1

        ring1_load()
        ring1_load()
        for ch in range(8):
            wada_chunk(ch, wring[ch % 2], ("wr", ch % 2))
            ring1_load()
        s.op("dve", lambda e: e.scalar_tensor_tensor(out=ABv[:, 0, :], in0=modsb[:, 16:32, 0], scalar=1.0, in1=vecs[:, 0:16], op0=ALU.add, op1=ALU.mult),
             reads=["modsb", "vecs"], writes=["AB0"])
        s.op("dve", lambda e: e.scalar_tensor_tensor(out=ABv[:, 1, :], in0=modsb[:, 16:32, 1], scalar=1.0, in1=vecs[:, 0:16], op0=ALU.add, op1=ALU.mult),
             reads=["modsb", "vecs"], writes=["AB1"])

        trp_pairs = [0, 1]
        cnt = {"sub": 0, "grp": 0}

        def stage_a1(src_ap):
            k = cnt["sub"]
            cnt["sub"] += 1
            sl = k % 2
            ss = stat[:, sl:sl + 1]
            rs = stat[:, 2 + sl:3 + sl]
            s.dma("sp", xt[sl], src_ap, "g_x%d" % sl, writes=[("xt", sl)])
            s.op("act", lambda e: e.activation(out=xn[sl], in_=xt[sl], func=AF.Square, accum_out=ss), reads=[("xt", sl)], writes=[("xn", sl), ("ss", sl)])
            s.op("act", lambda e: e.activation(out=ss, in_=ss, func=AF.Sqrt, scale=1.0 / D, bias=EPS), reads=[("ss", sl)], writes=[("ss", sl)])
            s.op("dve", lambda e: e.reciprocal(out=rs, in_=ss), reads=[("ss", sl)], writes=[("rs", sl)])
            s.op("act", lambda e: e.activation(out=xn[sl], in_=xt[sl], func=AF.Copy, scale=rs), reads=[("xt", sl), ("rs", sl)], writes=[("xn", sl)])
            return sl

        def stage_a2(sl, hb, col0, a_idx, b_col):
            sub = col0 // 128
            trp = P[0][:].bitcast(BF16).rearrange("p (j t) -> p j t", j=16)
            s.op("pe", [lambda e, j=j: e.transpose(out=trp[:, j, :], in_=xn[sl][:, j * 128:(j + 1) * 128], identity=ident) for j in range(16)],
                 reads=[("xn", sl), "ident"], writes=[("ps", 0), ("ps", 1)])
            for j in range(16):
                s.op("dve", lambda e, j=j: e.tensor_scalar(out=hxB[hb][:, j, col0:col0 + 128], in0=trp[:, j, :], scalar1=ABv[:, a_idx, j:j + 1],
                                                           scalar2=modsb[:, j, b_col:b_col + 1], op0=ALU.mult, op1=ALU.add),
                     reads=[("ps", j // 8), "AB%d" % a_idx, "modsb"], writes=[("hx", hb, sub, j)])

        def stage_a_all(srcs, hb, a_idx, b_col):
            sl_next = stage_a1(srcs[0])
            for k in range(len(srcs)):
                sl_cur = sl_next
                if k + 1 < len(srcs):
                    sl_next = stage_a1(srcs[k + 1])
                stage_a2(sl_cur, hb, k * 128, a_idx, b_col)

        def inproj_group(slot, slotkey, hb, col0):
            b = 2 + cnt["grp"] % 6
            cnt["grp"] += 1
            sub = col0 // 128
            s.op("pe", [lambda e, kc=kc: e.matmul(bank(b), lhsT=hxB[hb][:, kc, col0:col0 + 128], rhs=slot[:, kc, :], start=(kc == 0), stop=(kc == 15)) for kc in range(16)],
                 reads=[slotkey] + [("hx", hb, sub, kc) for kc in range(16)], writes=[("ps", b)])
            return b

        def qk_post(b, nh, gidx, use_rope, tglob, dst_ap, qi, vdst=None):
            q = qtmp[qi]
            K = lambda name: (name, qi)
            W = nh * 128
            pb = bank(b)[:, 0:W]
            ssq = stat[:, 8 + 4 * qi:8 + 4 * qi + nh]
            rq = stat[:, 32 + 4 * qi:32 + 4 * qi + nh]
            junk = q["junk"]
            s.op("act", [lambda e, h=h: e.activation(out=junk[:, h * 128:(h + 1) * 128], in_=pb[:, h * 128:(h + 1) * 128], func=AF.Square, accum_out=ssq[:, h:h + 1])
                         for h in range(nh)],
                 reads=[("ps", b)], writes=[K("t1"), K("t2"), K("ssq")])
            yield
            if vdst is not None:
                s.op("act", lambda e: e.activation(out=vdst, in_=bank(b)[:, 256:512], func=AF.Copy), reads=[("ps", b)], writes=[("V", qi)])
                yield
            s.op("act", lambda e: e.activation(out=ssq, in_=ssq, func=AF.Sqrt, scale=1.0 / 128, bias=EPS), reads=[K("ssq")], writes=[K("ssq")])
            yield
            s.op("dve", lambda e: e.reciprocal(out=rq, in_=ssq), reads=[K("ssq")], writes=[K("rq")])
            yield
            gname = "gq" if gidx == 0 else "gk"
            qr = q["qr"][:, 0:W]
            if use_rope:
                s.op("dve", [lambda e, h=h: e.scalar_tensor_tensor(out=q["qf"][:, h * 128:(h + 1) * 128], in0=pb[:, h * 128:(h + 1) * 128], scalar=rq[:, h:h + 1],
                                                                  in1=gqk[:, gidx, :], op0=ALU.mult, op1=ALU.mult) for h in range(nh)],
                     reads=[("ps", b), K("rq"), gname], writes=[K("qf")])
                yield "front_done"
                q4 = q["qf"][:, 0:W].rearrange("p (h i t) -> p h i t", h=nh, t=2)
                o4 = qr.rearrange("p (h i t) -> p h i t", h=nh, t=2)
                xa = q4[:, :, :, 0]
                xb = q4[:, :, :, 1]
                cosb = rope[:, 0, tglob, :].unsqueeze(1).to_broadcast([128, nh, 64])
                sinb = rope[:, 1, tglob, :].unsqueeze(1).to_broadcast([128, nh, 64])
                t1 = q["t1"][:, 0:nh, :]
                t2 = q["t2"][:, 0:nh, :]
                s.op("dve", lambda e: e.tensor_tensor(out=t1, in0=xa, in1=cosb, op=ALU.mult), reads=[K("qf"), "rope"], writes=[K("t1")])
                yield
                s.op("dve", lambda e: e.tensor_tensor(out=t2, in0=xb, in1=sinb, op=ALU.mult), reads=[K("qf"), "rope"], writes=[K("t2")])
                yield
                s.op("dve", lambda e: e.tensor_tensor(out=o4[:, :, :, 0], in0=t1, in1=t2, op=ALU.subtract), reads=[K("t1"), K("t2")], writes=[K("qr0")])
                yield
                s.op("dve", lambda e: e.tensor_tensor(out=t1, in0=xa, in1=sinb, op=ALU.mult), reads=[K("qf"), "rope"], writes=[K("t1")])
                yield
                s.op("dve", lambda e: e.tensor_tensor(out=t2, in0=xb, in1=cosb, op=ALU.mult), reads=[K("qf"), "rope"], writes=[K("t2")])
                yield
                s.op("dve", lambda e: e.tensor_tensor(out=o4[:, :, :, 1], in0=t1, in1=t2, op=ALU.add), reads=[K("t1"), K("t2")], writes=[K("qr1")])
                yield "rope_done"
            else:
                s.op("dve", [lambda e, h=h: e.scalar_tensor_tensor(out=qr[:, h * 128:(h + 1) * 128], in0=pb[:, h * 128:(h + 1) * 128], scalar=rq[:, h:h + 1],
                                                                  in1=gqk[:, gidx, :], op0=ALU.mult, op1=ALU.mult) for h in range(nh)],
                     reads=[("ps", b), K("rq"), gname], writes=[K("qr0"), K("qr1")])
                yield
            pbT = bank_bf(b, nh)
            s.op("pe", [lambda e, h=h: e.transpose(out=pbT[:, h, :], in_=qr[:, h * 128:(h + 1) * 128], identity=ident) for h in range(nh)],
                 reads=[K("qr0"), K("qr1"), "ident"], writes=[("ps", b)])
            yield
            s.op("act", lambda e: e.activation(out=dst_ap, in_=pbT, func=AF.Copy), reads=[("ps", b)], writes=[("qkT_out", qi)])
            yield

        def run_interleaved(gens, until=None):
            active = list(gens)
            while active:
                for g_ in list(active):
                    try:
                        r_ = next(g_)
                        if until is not None and r_ == until:
                            active.remove(g_)
                    except StopIteration:
                        active.remove(g_)

        stage_a_all([ctx_d[cs * 128:(cs + 1) * 128, :] for cs in range(2)], 1, 1, 1)
        stage_a_all([x_d[s4 * 128:(s4 + 1) * 128, :] for s4 in range(4)], 0, 0, 0)
        it = 8
        slot, skey = wring[it % 2], ("wr", it % 2)
        gl = []
        for cs in range(2):
            b = inproj_group(slot, skey, 1, cs * 128)
            gl.append(qk_post(b, 2, 1, False, 0, kT[:, :, cs * 128:(cs + 1) * 128], cs, vdst=V[:, cs, :]))
        ring1_load()
        it += 1
        pending = gl
        for m in range(4):
            hb = m % 2
            a2_todo = None
            for n in range(5):
                slot, skey = wring[it % 2], ("wr", it % 2)
                for pr_ in range(2):
                    gl = []
                    for dd in range(2):
                        s4 = pr_ * 2 + dd
                        tg = m * 4 + s4
                        b = inproj_group(slot, skey, hb, s4 * 128)
                        if n < 2:
                            s.op("act", lambda e, b=b, tg=tg, n=n: e.activation(out=U[:, tg, n * 512:(n + 1) * 512], in_=bank(b), func=AF.Copy),
                                 reads=[("ps", b)], writes=[("U", dd)])
                        elif n < 4:
                            h0 = (n - 2) * 4
                            gl.append(qk_post(b, 4, 0, True, tg, QT[:, h0:h0 + 4, tg * 128:(tg + 1) * 128], dd))
                        else:
                            gl.append(qk_post(b, 2, 1, True, tg, kT[:, :, CTX + tg * 128:CTX + (tg + 1) * 128], dd, vdst=V[:, 2 + tg, :]))
                    if pr_ == 1:
                        ring1_load()
                        it += 1
                    if n < 2:
                        run_interleaved(pending)
                        pending = []
                        if a2_todo is not None:
                            stage_a2(*a2_todo)
                            a2_todo = None
                        if m + 1 < 4:
                            sa = n * 2 + pr_
                            tgn = (m + 1) * 4 + sa
                            sl_ = stage_a1(x_d[tgn * 128:(tgn + 1) * 128, :])
                            a2_todo = (sl_, 1 - hb, sa * 128, 0, 0)
                    else:
                        if a2_todo is not None:
                            stage_a2(*a2_todo)
                            a2_todo = None
                        run_interleaved(gl, "front_done")
                        run_interleaved(pending)
                        run_interleaved(gl, "rope_done")
                        pending = gl
                if n == 4:
                    precast(1)
        run_interleaved(pending)

        s.barrier()
        if debug_stop == 1:
            s.dma("sp", dbg["qT"], QT, "g_dbg", reads=[])
            s.dma("sp", dbg["kT"], kT, "g_dbg")
            s.dma("sp", dbg["V"], V, "g_dbg")
            s.dma("sp", dbg["U"], U, "g_dbg")
            s.dma("sp", dbg["mod"], modsb, "g_dbg")
            s.final_wait("sp", ["g_dbg"])
            s.generate(nc)
            return nc

        s.dma("pool", poolw, pw_d, "g_c", writes=["poolw"])
        s.dma("sp", GB[:, 2, :], fn_d.partition_broadcast(128), "g_sm", writes=[("GB", 2)])
        ring2_items = list(range(8, 24))
        r2 = {"next": 0}

        def ring2_load():
            i = r2["next"]
            if i >= len(ring2_items):
                return
            s.dma("pool", wring2[i % 2], wada_d[ring2_items[i]], "g_r2_%d" % (i % 2), writes=[("wr2", i % 2)])
            r2["next"] += 1

        ring2_load()
        ring2_load()
        precast(1)

        pb_i = {"i": 0}

        def next_pbank():
            b = pb_i["i"] % 6
            pb_i["i"] += 1
            return b

        for g in range(4):
            w = POOL_WINDOWS[g]
            for i in range(4):
                psl = (g * 4 + i) % 2
                for kc in range(2):
                    b = next_pbank()
                    c0 = g * 256 + kc * 128
                    fns = []
                    for tb in range(4):
                        tblk = 4 * i + tb
                        outp = bank(b)[:, tb * 128:(tb + 1) * 128]
                        selfband = 3 if tblk == 0 else (4 if tblk == 15 else 0)
                        lst = [(tblk, selfband)]
                        if tblk > 0:
                            lst.append((tblk - 1, 1))
                        if tblk < 15:
                            lst.append((tblk + 1, 2))
                        for li, (sb_, bt) in enumerate(lst):
                            fns.append(lambda e, outp=outp, sb_=sb_, bt=bt, li=li, nl=len(lst), c0=c0, g=g:
                                       e.matmul(outp, lhsT=U[:, sb_, c0:c0 + 128], rhs=bands[:, g * 5 + bt, :], start=(li == 0), stop=(li == nl - 1)))
                    s.op("pe", fns, reads=["bands"], writes=[("ps", b)])
                    s.op("act", lambda e, b=b, psl=psl, kc=kc, w=w: e.activation(out=pooled[psl][:, kc, :], in_=bank(b), func=AF.Copy, scale=1.0 / w),
                         reads=[("ps", b)], writes=[("pooled", psl, kc)])
                    if i == 0:
                        s.op("dve", lambda e, b=b, psl=psl, kc=kc, g=g: e.tensor_tensor(out=pooled[psl][:, kc, 0:8], in0=bank(b)[:, 0:8], in1=pfix[:, g, 0, :], op=ALU.mult),
                             reads=[("ps", b), "pfix"], writes=[("pooled", psl, kc)])
                    if i == 3:
                        s.op("dve", lambda e, b=b, psl=psl, kc=kc, g=g: e.tensor_tensor(out=pooled[psl][:, kc, 504:512], in0=bank(b)[:, 504:512], in1=pfix[:, g, 1, :], op=ALU.mult),
                             reads=[("ps", b), "pfix"], writes=[("pooled", psl, kc)])
                for dc in range(2):
                    b = next_pbank()
                    s.op("pe", [lambda e, b=b, kc=kc, dc=dc, g=g, psl=psl: e.matmul(bank(b), lhsT=poolw[:, g, kc, dc * 128:(dc + 1) * 128], rhs=pooled[psl][:, kc, :],
                                                                                    start=(kc == 0), stop=(kc == 1)) for kc in range(2)],
                         reads=["poolw", ("pooled", psl, 0), ("pooled", psl, 1)], writes=[("ps", b)])
                    s.op("act", lambda e, b=b, g=g, dc=dc, i=i: e.activation(out=CP[:, g * 2 + dc, i * 512:(i + 1) * 512], in_=bank(b), func=AF.Copy,
                                                                             scale=vecs[:, 32 + g * 2 + dc:33 + g * 2 + dc]),
                         reads=[("ps", b), "vecs"], writes=[("CP", i)])
            precast(1)

        steps = []
        for g in range(2):
            for i in range(4):
                for hh in range(4):
                    for pr in range(9):
                        steps.append((g, i, 4 * g + hh, pr))
        OB, DB = 4, 5

        def emit_qk(si):
            g, i, h, pr = steps[si]
            pp = si % 2
            s.op("pe", [lambda e, c=c, g=g, h=h, i=i, pp=pp, pr=pr: e.matmul(P[pp][:, c * 512:(c + 1) * 512], lhsT=kT[:, g, (2 * pr + c) * 128:(2 * pr + c + 1) * 128],
                                                                                rhs=QT[:, h, i * 512:(i + 1) * 512], start=True, stop=True) for c in range(2)],
                 reads=[("q", h, i)], writes=[("ps", 2 * pp), ("ps", 2 * pp + 1)])
            es = si % 3
            s.op("act", lambda e, pp=pp, es=es: e.activation(out=E[es], in_=P[pp][:], func=AF.Exp), reads=[("ps", 2 * pp), ("ps", 2 * pp + 1)], writes=[("E", es)])

        def emit_pv(si):
            g, i, h, pr = steps[si]
            es = si % 3
            fns = []
            for c in range(2):
                cc_ = 2 * pr + c
                fns.append(lambda e, c=c, cc_=cc_, g=g, es=es: e.matmul(bank(OB), lhsT=V[:, cc_, g * 128:(g + 1) * 128], rhs=E[es][:, c * 512:(c + 1) * 512],
                                                                       start=(cc_ == 0), stop=(cc_ == 17)))
                fns.append(lambda e, c=c, cc_=cc_, es=es: e.matmul(bank(DB), lhsT=ones, rhs=E[es][:, c * 512:(c + 1) * 512], start=(cc_ == 0), stop=(cc_ == 17)))
            s.op("pe", fns, reads=[("E", es), "ones"], writes=[("ps", OB), ("ps", DB)])
            if pr == 8:
                s.op("act", lambda e: e.activation(out=Dsb, in_=bank(DB), func=AF.Copy), reads=[("ps", DB)], writes=["Dsb"])
                s.op("dve", lambda e: e.tensor_copy(out=Osb, in_=bank(OB)), reads=[("ps", OB)], writes=["Osb"])
                s.op("dve", lambda e: e.reciprocal(out=rD, in_=Dsb), reads=["Dsb"], writes=["rD"])
                s.op("dve", lambda e, h=h, i=i: e.tensor_tensor(out=QT[:, h, i * 512:(i + 1) * 512], in0=Osb, in1=rD, op=ALU.mult),
                     reads=["Osb", "rD"], writes=[("q", h, i)])

        wch = {"i": 0}

        def wada_rest_step():
            i = wch["i"]
            if i >= 16:
                return
            wada_chunk(8 + i, wring2[i % 2], ("wr2", i % 2))
            ring2_load()
            wch["i"] += 1

        emit_qk(0)
        for si in range(len(steps)):
            if si + 1 < len(steps):
                emit_qk(si + 1)
            emit_pv(si)
            if steps[si][3] == 8:
                itn = si // 9
                if itn % 2 == 1:
                    wada_rest_step()
                if itn % 4 == 3 and itn < 28:
                    precast(1)
        while wch["i"] < 16:
            wada_rest_step()
        s.op("dve", lambda e: e.scalar_tensor_tensor(out=ABv[:, 2, :], in0=modsb[:, 48:64, 0], scalar=1.0, in1=vecs[:, 16:32], op0=ALU.add, op1=ALU.mult),
             reads=["modsb", "vecs"], writes=["AB2"])

        s.barrier()
        if debug_stop == 2:
            s.dma("sp", dbg["qT"], QT, "g_dbg")
            s.dma("sp", dbg["CP"], CP, "g_dbg")
            s.dma("sp", dbg["GB"], GB, "g_dbg")
            s.dma("sp", dbg["mod"], modsb, "g_dbg")
            s.final_wait("sp", ["g_dbg"])
            s.generate(nc)
            return nc

        items3 = []
        for i in range(4):
            items3 += [("wo", k, wo_s) for k in range(8)] + [("wgu", k, wgu_s) for k in range(NF)] + [("wd", k, wd_s) for k in range(24)]
        r3 = {"next": 0, "cons": 0}

        def ring3_load():
            i = r3["next"]
            if i >= len(items3):
                return
            kind, k, src = items3[i]
            s.dma("sp", ring[i % 3], src[k], "g_r3_%d" % (i % 3), reads=[("scr", kind, k // 4)], writes=[("r3", i % 3)])
            r3["next"] += 1

        def ring3_take():
            i = r3["cons"]
            r3["cons"] += 1
            return ring[i % 3], ("r3", i % 3)

        ssp = stat[:, 40:56].rearrange("p (a b) -> p a b", a=4)

        def evac_residual(bset, n, gidx, tcount):
            for sq_ in range(4):
                tb = tmp3[tcount["i"] % 2]
                tk = ("tmp3", tcount["i"] % 2)
                jb = sg[tcount["i"] % 2]
                jk = ("sg", tcount["i"] % 2)
                tcount["i"] += 1
                ncols = slice(n * 512, (n + 1) * 512)
                s.op("dve", lambda e, tb=tb, b=bset + sq_, ncols=ncols: e.tensor_tensor(out=tb, in0=bank(b), in1=GB[:, gidx, ncols], op=ALU.mult),
                     reads=[("ps", bset + sq_)], writes=[tk])
                s.op("dve", lambda e, tb=tb, sq_=sq_, ncols=ncols: e.tensor_tensor(out=x1[:, sq_, ncols], in0=tb, in1=x1[:, sq_, ncols], op=ALU.add),
                     reads=[tk, ("x1", sq_)], writes=[("x1", sq_)])
                s.op("act", lambda e, jb=jb, sq_=sq_, ncols=ncols, n=n: e.activation(out=jb, in_=x1[:, sq_, ncols], func=AF.Square, accum_out=ssp[:, sq_, n:n + 1]),
                     reads=[("x1", sq_)], writes=[jk, ("ssp", sq_, n)])

        def rstd_from_partials(sq_):
            ss = stat[:, 16 + sq_:17 + sq_]
            rs = stat[:, 20 + sq_:21 + sq_]
            s.op("dve", lambda e: e.tensor_reduce(out=ss, in_=ssp[:, sq_, :], axis=AX.X, op=ALU.add), reads=[("ssp", sq_, n_) for n_ in range(4)], writes=[("ss3", sq_)])
            s.op("act", lambda e: e.activation(out=ss, in_=ss, func=AF.Sqrt, scale=1.0 / D, bias=EPS), reads=[("ss3", sq_)], writes=[("ss3", sq_)])
            s.op("dve", lambda e: e.reciprocal(out=rs, in_=ss), reads=[("ss3", sq_)], writes=[("rs3", sq_)])
            return rs

        for _ in range(3):
            ring3_load()
        tcount = {"i": 0}
        for i in range(4):
            c0 = i * 512
            for q_ in range(4):
                s.dma("pool", x1[:, q_, :], x_d[c0 + q_ * 128:c0 + (q_ + 1) * 128, :], "g_x3%d" % (q_ % 2), writes=[("x1", q_)])
            for n in range(4):
                bset = 0 if n % 2 == 0 else 4
                for kh in range(2):
                    slot, skey = ring3_take()
                    w3 = slot.rearrange("p (k c) -> p k c", k=8)
                    for sq_ in range(4):
                        s.op("pe", [lambda e, kc=kc, sq_=sq_, kh=kh, bset=bset, w3=w3, c0=c0: e.matmul(bank(bset + sq_), lhsT=cat_chunk(kh * 8 + kc)[:, c0 + sq_ * 128:c0 + (sq_ + 1) * 128],
                                                                                               rhs=w3[:, kc, :], start=(kh == 0 and kc == 0), stop=(kh == 1 and kc == 7))
                                    for kc in range(8)],
                             reads=[skey, ("cat", i)], writes=[("ps", bset + sq_)])
                    ring3_load()
                evac_residual(bset, n, 0, tcount)
            rss = [rstd_from_partials(sq_) for sq_ in range(4)]

            def hf_a1(sq_):
                xb = sq_ % 2
                s.op("act", lambda e, sq_=sq_, xb=xb: e.activation(out=xn3[xb], in_=x1[:, sq_, :], func=AF.Copy, scale=rss[sq_]),
                     reads=[("x1", sq_), ("rs3", sq_)], writes=[("xn3", xb)])

            def hf_a2(sq_, c0=c0, i=i):
                xb = sq_ % 2
                pr = sq_ % 2
                trp = P[pr][:].bitcast(BF16).rearrange("p (j t) -> p j t", j=16)
                s.op("pe", [lambda e, j=j, trp=trp, xb=xb: e.transpose(out=trp[:, j, :], in_=xn3[xb][:, j * 128:(j + 1) * 128], identity=ident) for j in range(16)],
                     reads=[("xn3", xb), "ident"], writes=[("ps", 2 * pr), ("ps", 2 * pr + 1)])
                for j in range(16):
                    s.op("dve", lambda e, j=j, trp=trp, sq_=sq_, c0=c0: e.tensor_scalar(out=cat_chunk(j)[:, c0 + sq_ * 128:c0 + (sq_ + 1) * 128], in0=trp[:, j, :],
                                                                                        scalar1=ABv[:, 2, j:j + 1], scalar2=modsb[:, 32 + j, 0:1], op0=ALU.mult, op1=ALU.add),
                         reads=[("ps", 2 * pr + j // 8), "AB2", "modsb"], writes=[("hf", sq_, j)], wars=[("cat", i)])

            hf_a1(0)
            for sq_ in range(4):
                if sq_ + 1 < 4:
                    hf_a1(sq_ + 1)
                hf_a2(sq_)
            for f in range(NF):
                slot, skey = ring3_take()
                w4 = slot.rearrange("p (g k c) -> p g k c", g=2, k=16)
                gb_, ub_ = 2 * (f % 4), 2 * (f % 4) + 1
                s.op("pe", [lambda e, kc=kc, w4=w4, gb_=gb_, c0=c0: e.matmul(bank(gb_), lhsT=w4[:, 0, kc, :], rhs=cat_chunk(kc)[:, c0:c0 + 512], start=(kc == 0), stop=(kc == 15)) for kc in range(16)]
                     + [lambda e, kc=kc, w4=w4, ub_=ub_, c0=c0: e.matmul(bank(ub_), lhsT=w4[:, 1, kc, :], rhs=cat_chunk(kc)[:, c0:c0 + 512], start=(kc == 0), stop=(kc == 15)) for kc in range(16)],
                     reads=[skey] + [("hf", q_, kc) for q_ in range(4) for kc in range(16)], writes=[("ps", gb_), ("ps", ub_)])
                ring3_load()
                if i == 0 and f in (8, 16, 24):
                    precast(1)
                sgb = sg[f % 2]
                s.op("act", lambda e, sgb=sgb, gb_=gb_: e.activation(out=sgb, in_=bank(gb_), func=AF.Silu), reads=[("ps", gb_)], writes=[("sg", f % 2)])
                s.op("dve", lambda e, sgb=sgb, ub_=ub_, f=f: e.tensor_tensor(out=actT[:, f, :], in0=bank(ub_), in1=sgb, op=ALU.mult),
                     reads=[("ps", ub_), ("sg", f % 2)], writes=[("act", f)])
            for n in range(4):
                bset = 0 if n % 2 == 0 else 4
                for j in range(6):
                    slot, skey = ring3_take()
                    w3 = slot.rearrange("p (k c) -> p k c", k=8)
                    nfl = 8 if j < 5 else 4
                    for sq_ in range(4):
                        s.op("pe", [lambda e, fl=fl, j=j, sq_=sq_, bset=bset, w3=w3: e.matmul(bank(bset + sq_), lhsT=actT[:, 8 * j + fl, sq_ * 128:(sq_ + 1) * 128], rhs=w3[:, fl, :],
                                                                                             start=(j == 0 and fl == 0), stop=(8 * j + fl == NF - 1)) for fl in range(nfl)],
                             reads=[skey] + [("act", 8 * j + fl) for fl in range(nfl)], writes=[("ps", bset + sq_)])
                    ring3_load()
                evac_residual(bset, n, 1, tcount)
            for sq_ in range(4):
                rs = rstd_from_partials(sq_)
                s.op("dve", lambda e, sq_=sq_, rs=rs: e.scalar_tensor_tensor(out=x1[:, sq_, :], in0=x1[:, sq_, :], scalar=rs, in1=GB[:, 2, :], op0=ALU.mult, op1=ALU.mult),
                     reads=[("x1", sq_), ("rs3", sq_)], writes=[("x1", sq_)])
                s.dma("pool", out_d[c0 + sq_ * 128:c0 + (sq_ + 1) * 128, :], x1[:, sq_, :], "g_st%d" % (sq_ % 2), reads=[("x1", sq_)])

        s.final_wait("pool", ["g_st0", "g_st1"])
        s.generate(nc)
    return nc


def _pool_consts():
    cm = np.zeros((128, 22, 128), np.float32)
    cm[:, 0, :] = np.eye(128, dtype=np.float32)
    cm[:, 1, :] = 1.0
    pfix = np.zeros((128, 4, 2, 8), np.float32)
    sidx = np.arange(128)[:, None]
    tidx = np.arange(128)[None, :]
    for g, w in enumerate(POOL_WINDOWS):
        h = w // 2
        inwin = ((sidx >= tidx - h) & (sidx < tidx + h)).astype(np.float32)
        eye = np.eye(128, dtype=np.float32)
        self_ = inwin - w * eye
        prev = (sidx - 128 >= tidx - h).astype(np.float32)
        nxt = (sidx + 128 < tidx + h).astype(np.float32)
        cnt_first = np.minimum(np.arange(128) + h, T) - np.maximum(np.arange(128) - h, 0)
        first = inwin - np.diag(cnt_first.astype(np.float32))
        tl = np.arange(128) + (T - 128)
        cnt_last = np.minimum(tl + h, T) - np.maximum(tl - h, 0)
        last = inwin - np.diag(cnt_last.astype(np.float32))
        for k, mtx in enumerate((self_, prev, nxt, first, last)):
            cm[:, 2 + g * 5 + k, :] = mtx
        pfix[:, g, 0, :] = 1.0 / cnt_first[0:8]
        pfix[:, g, 1, :] = 1.0 / cnt_last[120:128]
    return cm, pfix


def _rope_tables():
    n_rows = T // 64
    rows = np.repeat(np.arange(n_rows, dtype=np.float32), 64)
    cols = np.tile(np.arange(64, dtype=np.float32), n_rows)
    freqs = (np.float32(10000.0) ** (-np.arange(0, 64, 2, dtype=np.float32) / np.float32(64))).astype(np.float32)
    ang = np.concatenate([rows[:, None] * freqs, cols[:, None] * freqs], axis=-1).astype(np.float32)
    cos = np.cos(ang).astype(np.float32).reshape(16, 128, 64).transpose(1, 0, 2)
    sin = np.sin(ang).astype(np.float32).reshape(16, 128, 64).transpose(1, 0, 2)
    return np.ascontiguousarray(np.stack([cos, sin], axis=1))


def _prep_shared(c_ctx, w_ada, b_ada, norm_mix, norm_ffn, w_in, pool_w, pool_scale, q_norm, k_norm, w_out, w_gate, w_up, w_down, final_norm):
    f = np.float32
    w_ada = np.asarray(w_ada, f)[0]
    wada = np.zeros((24, 128, 17, 512), f)
    wada[:, :, 0:16, :] = w_ada.reshape(16, 128, 24, 512).transpose(2, 1, 0, 3)
    wada[:, 0, 16, :] = np.asarray(b_ada, f)[0].reshape(24, 512)
    vecs = np.zeros((128, 40), f)
    vecs[:, 0:16] = np.asarray(norm_mix, f)[0].reshape(16, 128).T
    vecs[:, 16:32] = np.asarray(norm_ffn, f)[0].reshape(16, 128).T
    vecs[:, 32:40] = np.asarray(pool_scale, f)[0].reshape(8, 128).T
    win = np.ascontiguousarray(np.asarray(w_in, f)[0].reshape(16, 128, 5, 512).transpose(2, 1, 0, 3))
    pw = np.ascontiguousarray(np.asarray(pool_w, f)[0].reshape(4, 2, 128, 256).transpose(2, 0, 1, 3))
    wo = np.asarray(w_out, f)[0].reshape(2, 8, 128, 4, 512).transpose(3, 0, 2, 1, 4)
    wo = np.ascontiguousarray(wo).reshape(8, 128, 4096)
    wg = np.asarray(w_gate, f)[0].reshape(16, 128, NF, 128).transpose(2, 1, 0, 3)
    wu = np.asarray(w_up, f)[0].reshape(16, 128, NF, 128).transpose(2, 1, 0, 3)
    wgu = np.ascontiguousarray(np.stack([wg, wu], axis=2)).reshape(NF, 128, 4096)
    wdn = np.asarray(w_down, f)[0].reshape(NF, 128, 4, 512)
    wd = np.zeros((4, 6, 128, 8, 512), f)
    for j in range(6):
        nfl = 8 if j < 5 else 4
        wd[:, j, :, 0:nfl, :] = wdn[8 * j:8 * j + nfl].transpose(2, 1, 0, 3)
    wd = wd.reshape(24, 128, 4096)
    cm, pfix = _pool_consts()
    return dict(wada=wada, vecs=vecs, gq=np.ascontiguousarray(np.asarray(q_norm, f)[0]), gk=np.ascontiguousarray(np.asarray(k_norm, f)[0]),
                fn=np.ascontiguousarray(np.asarray(final_norm, f)), win=win, pw=pw, wo=wo, wgu=wgu, wd=wd, cmat=cm, rope=_rope_tables(), pfix=pfix)


def make_in_maps(x, c, ctx, c_ctx, **wts):
    shared = _prep_shared(c_ctx, **wts)
    x = np.asarray(x, np.float32)
    c = np.asarray(c, np.float32)
    ctx = np.asarray(ctx, np.float32)
    c_ctx = np.asarray(c_ctx, np.float32)
    maps = []
    for b in range(8):
        cc = np.ascontiguousarray(np.stack([c[b].reshape(16, 128).T, c_ctx.reshape(16, 128).T], axis=1))
        m = dict(shared)
        m["x"] = np.ascontiguousarray(x[b])
        m["ctx"] = np.ascontiguousarray(ctx[b])
        m["cc"] = cc
        maps.append(m)
    return maps


def kernel(x, c, ctx, c_ctx, w_ada, b_ada, norm_mix, norm_ffn, w_in, pool_w, pool_scale, q_norm, k_norm,
           w_out, w_gate, w_up, w_down, final_norm):
    in_maps = make_in_maps(x, c, ctx, c_ctx, w_ada=w_ada, b_ada=b_ada, norm_mix=norm_mix, norm_ffn=norm_ffn, w_in=w_in,
                           pool_w=pool_w, pool_scale=pool_scale, q_norm=q_norm, k_norm=k_norm, w_out=w_out,
                           w_gate=w_gate, w_up=w_up, w_down=w_down, final_norm=final_norm)
    nc = build_program(DEBUG_STOP)
    res = run_bass_kernel_spmd(nc, in_maps, core_ids=list(range(8)))
    if DEBUG_STOP is not None:
        return res
    return np.stack([np.asarray(r["out"], np.float32) for r in res.results], axis=0)
```

```python
import contextlib
import math

import numpy as np

import concourse.bass as bass
import concourse.mybir as mybir
from concourse.bass_utils import run_bass_kernel_spmd

F32 = mybir.dt.float32
BF16 = mybir.dt.bfloat16
AF = mybir.ActivationFunctionType
ALU = mybir.AluOpType
AX = mybir.AxisListType

T = 2048
D = 2048
NT = 16
CTX = 256
DFF = 5632
NF = 44
EPS = 1e-6
POOL_WINDOWS = (2, 4, 8, 16)
ENGS = ("pe", "act", "dve", "pool", "sp")

DEBUG_STOP = None


class Sched:
    def __init__(self):
        self.prog = {e: [] for e in ENGS}
        self.cnt = {}
        self.lastw = {}
        self.readers = {}
        self.seen = {e: {} for e in ENGS}
        self.bg_groups = set()

    def _collect(self, reads, writes):
        toks = {}

        def add(k, v):
            if toks.get(k, 0) < v:
                toks[k] = v

        for r in reads:
            t = self.lastw.get(r)
            if t is not None:
                add(*t)
        for w in writes:
            t = self.lastw.get(w)
            if t is not None:
                add(*t)
            for k, v in self.readers.get(w, {}).items():
                add(k, v)
        return toks

    def _emit_waits(self, eng, toks, skip_self):
        for k in sorted(toks):
            v = toks[k]
            if k == eng and skip_self:
                continue
            if self.seen[eng].get(k, 0) >= v:
                continue
            self.seen[eng][k] = v
            self.prog[eng].append(("wait", k, v))

    def _record(self, tok, reads, writes):
        k, v = tok
        for r in reads:
            d = self.readers.setdefault(r, {})
            if d.get(k, 0) < v:
                d[k] = v
        for w in writes:
            self.lastw[w] = tok
            self.readers[w] = {}

    def op(self, eng, fns, reads=(), writes=(), wars=()):
        if callable(fns):
            fns = [fns]
        toks = self._collect(reads, tuple(writes) + tuple(wars))
        self._emit_waits(eng, toks, skip_self=(eng == "pe"))
        self.cnt[eng] = self.cnt.get(eng, 0) + 1
        tok = (eng, self.cnt[eng])
        self.prog[eng].append(("op", fns, eng, 1))
        self._record(tok, reads, writes)
        return tok

    def dma(self, queue, out, in_, group, reads=(), writes=(), background=False):
        if group not in self.cnt:
            self.cnt[group] = 0
            if background:
                self.bg_groups.add(group)
        toks = self._collect(reads, writes)
        if self.cnt[group] > 0 and toks.get(group, 0) < self.cnt[group]:
            toks[group] = self.cnt[group]
        self._emit_waits(queue, toks, skip_self=False)
        self.cnt[group] += 16
        tok = (group, self.cnt[group])
        self.prog[queue].append(("op", [lambda e, o=out, i=in_: e.dma_start(out=o, in_=i)], group, 16))
        self._record(tok, reads, writes)
        return tok

    def barrier(self):
        toks = {k: v for k, v in self.cnt.items() if v > 0 and k not in self.bg_groups}
        for e in ENGS:
            self._emit_waits(e, dict(toks), skip_self=(e == "pe"))
        self.lastw = {r: t for r, t in self.lastw.items() if t[0] in self.bg_groups}
        self.readers = {}

    def final_wait(self, eng, groups):
        toks = {g: self.cnt[g] for g in groups if self.cnt.get(g, 0) > 0}
        self._emit_waits(eng, toks, skip_self=False)

    def generate(self, nc):
        keys = [k for k, v in self.cnt.items() if v > 0]
        with contextlib.ExitStack() as st:
            sems = {k: st.enter_context(nc.semaphore("s_" + str(k))) for k in keys}
            block = st.enter_context(nc.Block())

            def run(engname):
                def body(e):
                    for item in self.prog[engname]:
                        if item[0] == "wait":
                            e.wait_ge(sems[item[1]], item[2])
                        else:
                            _, fns, semk, inc = item
                            ins = None
                            for f in fns:
                                ins = f(e)
                            ins.then_inc(sems[semk], inc)
                return body

            block.tensor(run("pe"))
            block.scalar(run("act"))
            block.vector(run("dve"))
            block.gpsimd(run("pool"))
            block.sync(run("sp"))


def build_program(debug_stop=None):
    nc = bass.Bass("TRN2", target_bir_lowering=False)

    def din(name, shape):
        return nc.dram_tensor(name, list(shape), F32, kind="ExternalInput").ap()

    x_d = din("x", [T, D])
    ctx_d = din("ctx", [CTX, D])
    cc_d = din("cc", [128, 2, 16])
    wada_d = din("wada", [24, 128, 17, 512])
    vecs_d = din("vecs", [128, 40])
    gq_d = din("gq", [128])
    gk_d = din("gk", [128])
    fn_d = din("fn", [D])
    win_d = din("win", [5, 128, 16, 512])
    pw_d = din("pw", [128, 4, 2, 256])
    wo_d = din("wo", [8, 128, 4096])
    wgu_d = din("wgu", [NF, 128, 4096])
    wd_d = din("wd", [24, 128, 4096])
    cmat_d = din("cmat", [128, 22, 128])
    rope_d = din("rope", [128, 2, 16, 64])
    pfix_d = din("pfix", [128, 4, 2, 8])
    out_d = nc.dram_tensor("out", [T, D], F32, kind="ExternalOutput").ap()
    wo_s = nc.dram_tensor("wo_s", [8, 128, 4096], BF16, kind="Internal").ap()
    wgu_s = nc.dram_tensor("wgu_s", [NF, 128, 4096], BF16, kind="Internal").ap()
    wd_s = nc.dram_tensor("wd_s", [24, 128, 4096], BF16, kind="Internal").ap()
    dbg = {}
    if debug_stop is not None:
        dbg["qT"] = nc.dram_tensor("dbg_qT", [128, 8, T], BF16, kind="ExternalOutput").ap()
        dbg["kT"] = nc.dram_tensor("dbg_kT", [128, 2, T + CTX], BF16, kind="ExternalOutput").ap()
        dbg["V"] = nc.dram_tensor("dbg_V", [128, 18, 256], BF16, kind="ExternalOutput").ap()
        dbg["U"] = nc.dram_tensor("dbg_U", [128, 16, 1024], BF16, kind="ExternalOutput").ap()
        dbg["CP"] = nc.dram_tensor("dbg_CP", [128, 8, T], BF16, kind="ExternalOutput").ap()
        dbg["GB"] = nc.dram_tensor("dbg_GB", [128, 3, D], F32, kind="ExternalOutput").ap()
        dbg["mod"] = nc.dram_tensor("dbg_mod", [128, 64, 2], F32, kind="ExternalOutput").ap()

    s = Sched()
    with contextlib.ExitStack() as st:
        ARENA_W = 53200
        arena = st.enter_context(nc.sbuf_tensor("arena", [128, ARENA_W], F32))
        P = [st.enter_context(nc.psum_tensor("ps%d" % i, [128, 1024], F32)) for i in range(4)]

        def view(off, shape, dt):
            esz = 4 if dt == F32 else 2
            n = 1
            for d_ in shape[1:]:
                n *= d_
            nb = n * esz
            assert off % 4 == 0 and nb % 4 == 0 and off + nb <= ARENA_W * 4, (off, shape)
            a = arena[:, off // 4:(off + nb) // 4]
            if dt != F32:
                a = a.bitcast(dt)
            if len(shape) == 3:
                a = a.rearrange("p (a b) -> p a b", a=shape[1])
            elif len(shape) == 4:
                a = a.rearrange("p (a b c) -> p a b c", a=shape[1], b=shape[2])
            return a

        def bank(b):
            return P[b // 2][:, (b % 2) * 512:(b % 2) * 512 + 512]

        def bank_bf(b, n):
            return bank(b).bitcast(BF16)[:, 0:n * 128].rearrange("p (a b) -> p a b", a=n)

        O_QT = 0
        O_CP = 32768
        O_GB = 65536
        O_SM = 90112
        O_OV = 93696
        QT = view(O_QT, [128, 8, T], BF16)
        CP = view(O_CP, [128, 8, T], BF16)
        GB = view(O_GB, [128, 3, D], F32)
        o = O_SM
        ident = view(o, [128, 128], BF16); o += 256
        ones = view(o, [128, 128], BF16); o += 256
        vecs = view(o, [128, 40], F32); o += 160
        ABv = view(o, [128, 3, 16], F32); o += 192
        modsb = view(o, [128, 64, 2], F32); o += 512
        gqk = view(o, [128, 2, 128], F32); o += 1024
        sc2 = view(o, [128, 16, 2], BF16); o += 64
        cc = view(o, [128, 2, 16], F32); o += 128
        stat = view(o, [128, 64], F32); o += 256
        pfix = view(o, [128, 4, 2, 8], F32); o += 256
        assert o <= O_OV

        def cat_chunk(j):
            return CP[:, j, :] if j < 8 else QT[:, j - 8, :]

        o = O_OV
        U = view(o, [128, 16, 1024], BF16); o += 32768
        kT = view(o, [128, 2, T + CTX], BF16); o += 9216
        V = view(o, [128, 18, 256], BF16); o += 9216
        bands = view(o, [128, 20, 128], BF16); o += 5120
        scbc = view(o, [128, 16, 128], BF16); o += 4096
        O_OV2 = o
        wring = [view(o + i * 17408, [128, 17, 512], BF16) for i in range(2)]; o += 34816
        xt = [view(o + i * 8192, [128, D], F32) for i in range(2)]; o += 16384
        assert o <= ARENA_W * 4
        hxB = [view(O_CP + i * 16384, [128, 16, 512], BF16) for i in range(2)]
        o = O_GB
        xn = [view(o + i * 4096, [128, D], BF16) for i in range(2)]; o += 8192
        QTMP = 5120

        def mk_qtmp(b0):
            return dict(qr=view(b0, [128, 512], BF16), t1=view(b0 + 1024, [128, 4, 64], F32), junk=view(b0 + 1024, [128, 512], F32),
                        t2=view(b0 + 2048, [128, 4, 64], F32), qf=view(b0 + 3072, [128, 512], F32))

        qtmp = [mk_qtmp(o), mk_qtmp(O_OV2 + 34816 + 16384)]
        assert O_OV2 + 34816 + 16384 + QTMP <= ARENA_W * 4
        o += QTMP
        rope = view(o, [128, 2, 16, 64], F32); o += 8192
        assert o <= O_SM
        o = O_OV2
        E = [view(o + i * 2048, [128, 1024], BF16) for i in range(3)]; o += 6144
        rD = view(o, [128, 512], F32); o += 2048
        pooled = [view(o + i * 2048, [128, 2, 512], BF16) for i in range(2)]; o += 4096
        poolw = view(o, [128, 4, 2, 256], BF16); o += 4096
        wring2 = [view(o + i * 17408, [128, 17, 512], BF16) for i in range(2)]; o += 34816
        Osb = view(o, [128, 512], F32); o += 2048
        Dsb = view(o, [128, 512], F32); o += 2048
        dgt = view(o, [128, 4, 128], F32); o += 2048
        assert o <= ARENA_W * 4
        o = O_OV
        x1 = view(o, [128, 4, D], F32); o += 32768
        actT = view(o, [128, NF, 512], BF16); o += 45056
        ring = [view(o + i * 8192, [128, 4096], BF16) for i in range(3)]; o += 24576
        xn3 = [view(o + i * 4096, [128, D], BF16) for i in range(2)]; o += 8192
        sg = [view(o + i * 2048, [128, 512], F32) for i in range(2)]; o += 4096
        tmp3 = [view(o + i * 2048, [128, 512], F32) for i in range(2)]; o += 4096
        assert o <= ARENA_W * 4, o

        s.dma("sp", cc, cc_d, "g_sm", writes=["cc"])
        s.dma("sp", vecs, vecs_d, "g_sm", writes=["vecs"])
        s.dma("sp", gqk[:, 0, :], gq_d.partition_broadcast(128), "g_sm", writes=["gq"])
        s.dma("sp", gqk[:, 1, :], gk_d.partition_broadcast(128), "g_sm", writes=["gk"])
        s.dma("sp", rope, rope_d, "g_sm", writes=["rope"])
        s.dma("sp", pfix, pfix_d, "g_sm", writes=["pfix"])

        s.op("act", lambda e: e.activation(out=sc2[:, :, 0], in_=cc[:, 0, :], func=AF.Silu), reads=["cc"], writes=["sc2a"])
        s.op("act", lambda e: e.activation(out=sc2[:, :, 1], in_=cc[:, 1, :], func=AF.Silu), reads=["cc"], writes=["sc2b"])
        s.op("dve", lambda e: e.tensor_copy(out=scbc, in_=sc2[:, :, 0:1].to_broadcast([128, 16, 128])), reads=["sc2a"], writes=["scbc"])
        s.op("dve", lambda e: e.tensor_scalar(out=gqk[:, 0, :], in0=gqk[:, 0, :], scalar1=1.0 / math.sqrt(128.0), scalar2=0.0, op0=ALU.mult, op1=ALU.add),
             reads=["gq"], writes=["gq"])

        pc_list = []
        for kind, src, dst, n in (("wo", wo_d, wo_s, 8), ("wgu", wgu_d, wgu_s, NF), ("wd", wd_d, wd_s, 24)):
            for c in range(n // 4):
                pc_list.append((kind, c, src, dst))
        pc_state = {"i": 0}

        def precast(k=1):
            for _ in range(k):
                i = pc_state["i"]
                if i >= len(pc_list):
                    return
                kind, c, src, dst = pc_list[i]
                s.dma("pool", dst[4 * c:4 * c + 4].rearrange("a p f -> (a p) f"), src[4 * c:4 * c + 4].rearrange("a p f -> (a p) f"),
                      "g_pc%d" % (i % 3), writes=[("scr", kind, c)], background=True)
                pc_state["i"] += 1

        MODBANK = 6
        GBANK = 7
        modps = bank(MODBANK)[:, 0:128].rearrange("p (a b) -> p a b", b=2)

        def wada_chunk(ch, slot, slotkey):
            part = ch // 4
            if part in (2, 5):
                gi = 0 if part == 2 else 1
                cols = slice((ch % 4) * 512, (ch % 4) * 512 + 512)
                fns = [lambda e, kc=kc: e.matmul(bank(GBANK), lhsT=scbc[:, kc, :], rhs=slot[:, kc, :], start=(kc == 0), stop=False) for kc in range(16)]
                fns.append(lambda e: e.matmul(bank(GBANK), lhsT=ones[0:1, :], rhs=slot[0:1, 16, :], start=False, stop=True))
                s.op("pe", fns, reads=[slotkey, "scbc", "ones"], writes=[("ps", GBANK)])
                s.op("act", lambda e: e.activation(out=GB[:, gi, cols], in_=bank(GBANK), func=AF.Copy), reads=[("ps", GBANK)], writes=[("GB", gi, ch % 4)])
            elif part in (3, 4):
                base = {3: 32, 4: 48}[part] + (ch % 4) * 4
                fns = [lambda e, kc=kc: e.matmul(bank(GBANK), lhsT=scbc[:, kc, :], rhs=slot[:, kc, :], start=(kc == 0), stop=False) for kc in range(16)]
                fns.append(lambda e: e.matmul(bank(GBANK), lhsT=ones[0:1, :], rhs=slot[0:1, 16, :], start=False, stop=True))
                s.op("pe", fns, reads=[slotkey, "scbc", "ones"], writes=[("ps", GBANK)])
                s.op("dve", lambda e: e.tensor_tensor(out=dgt, in0=bank(GBANK).rearrange("p (a b) -> p a b", a=4), in1=ident.unsqueeze(1).to_broadcast([128, 4, 128]), op=ALU.mult),
                     reads=[("ps", GBANK), "ident"], writes=["dgt"])
                s.op("dve", lambda e: e.tensor_reduce(out=modsb[:, base:base + 4, 0], in_=dgt, axis=AX.X, op=ALU.add), reads=["dgt"], writes=["modsb"])
            else:
                base = {0: 0, 1: 16, 3: 32, 4: 48}[part] + (ch % 4) * 4
                fns = []
                for blk in range(4):
                    dst = modps[:, base + blk, :]
                    for kc in range(16):
                        fns.append(lambda e, kc=kc, blk=blk, dst=dst: e.matmul(dst, lhsT=slot[:, kc, blk * 128:(blk + 1) * 128], rhs=sc2[:, kc, :],
                                                                               start=(kc == 0), stop=False))
                    fns.append(lambda e, blk=blk, dst=dst: e.matmul(dst, lhsT=slot[0:1, 16, blk * 128:(blk + 1) * 128], rhs=ones[0:1, 0:2], start=False, stop=True))
                s.op("pe", fns, reads=[slotkey, "sc2a", "sc2b", "ones"], writes=[("ps", MODBANK)])
                s.op("dve", lambda e: e.tensor_copy(out=modsb[:, base:base + 4, :], in_=modps[:, base:base + 4, :]), reads=[("ps", MODBANK)], writes=["modsb"])

        ring1_items = [("wada", ch) for ch in range(8)] + [("win", 4)] + [("win", n) for _ in range(4) for n in range(5)]
        r1 = {"next": 0}

        def ring1_load():
            i = r1["next"]
            if i >= len(ring1_items):
                return
            kind, idx = ring1_items[i]
            slot = wring[i % 2]
            if kind == "wada":
                s.dma("pool", slot, wada_d[idx], "g_r1_%d" % (i % 2), writes=[("wr", i % 2)])
            else:
                s.dma("pool", slot[:, 0:16, :], win_d[idx], "g_r1_%d" % (i % 2), writes=[("wr", i % 2)])
            r1["next"] += 1

        ring1_load()
        ring1_load()
        s.dma("pool", ones, cmat_d[:, 1, :], "g_c", writes=["ones"])
        s.dma("pool", ident, cmat_d[:, 0, :], "g_c", writes=["ident"])
        s.dma("pool", bands, cmat_d[:, 2:22, :], "g_c", writes=["bands"])
        for ch in range(8):
            wada_chunk(ch, wring[ch % 2], ("wr", ch % 2))
            ring1_load()
        s.op("dve", lambda e: e.scalar_tensor_tensor(out=ABv[:, 0, :], in0=modsb[:, 16:32, 0], scalar=1.0, in1=vecs[:, 0:16], op0=ALU.add, op1=ALU.mult),
             reads=["modsb", "vecs"], writes=["AB0"])
        s.op("dve", lambda e: e.scalar_tensor_tensor(out=ABv[:, 1, :], in0=modsb[:, 16:32, 1], scalar=1.0, in1=vecs[:, 0:16], op0=ALU.add, op1=ALU.mult),
             reads=["modsb", "vecs"], writes=["AB1"])

        trp_pairs = [0, 1]
        cnt = {"sub": 0, "grp": 0}

        def stage_a1(src_ap):
            k = cnt["sub"]
            cnt["sub"] += 1
            sl = k % 2
            ss = stat[:, sl:sl + 1]
            rs = stat[:, 2 + sl:3 + sl]
            s.dma("sp", xt[sl], src_ap, "g_x%d" % sl, writes=[("xt", sl)])
            s.op("act", lambda e: e.activation(out=xn[sl], in_=xt[sl], func=AF.Square, accum_out=ss), reads=[("xt", sl)], writes=[("xn", sl), ("ss", sl)])
            s.op("act", lambda e: e.activation(out=ss, in_=ss, func=AF.Sqrt, scale=1.0 / D, bias=EPS), reads=[("ss", sl)], writes=[("ss", sl)])
            s.op("dve", lambda e: e.reciprocal(out=rs, in_=ss), reads=[("ss", sl)], writes=[("rs", sl)])
            s.op("act", lambda e: e.activation(out=xn[sl], in_=xt[sl], func=AF.Copy, scale=rs), reads=[("xt", sl), ("rs", sl)], writes=[("xn", sl)])
            return sl

        def stage_a2(sl, hb, col0, a_idx, b_col):
            sub = col0 // 128
            trp = P[0][:].bitcast(BF16).rearrange("p (j t) -> p j t", j=16)
            s.op("pe", [lambda e, j=j: e.transpose(out=trp[:, j, :], in_=xn[sl][:, j * 128:(j + 1) * 128], identity=ident) for j in range(16)],
                 reads=[("xn", sl), "ident"], writes=[("ps", 0), ("ps", 1)])
            for j in range(16):
                s.op("dve", lambda e, j=j: e.tensor_scalar(out=hxB[hb][:, j, col0:col0 + 128], in0=trp[:, j, :], scalar1=ABv[:, a_idx, j:j + 1],
                                                           scalar2=modsb[:, j, b_col:b_col + 1], op0=ALU.mult, op1=ALU.add),
                     reads=[("ps", j // 8), "AB%d" % a_idx, "modsb"], writes=[("hx", hb, sub, j)])

        def stage_a_all(srcs, hb, a_idx, b_col):
            sl_next = stage_a1(srcs[0])
            for k in range(len(srcs)):
                sl_cur = sl_next
                if k + 1 < len(srcs):
                    sl_next = stage_a1(srcs[k + 1])
                stage_a2(sl_cur, hb, k * 128, a_idx, b_col)

        def inproj_group(slot, slotkey, hb, col0):
            b = 2 + cnt["grp"] % 6
            cnt["grp"] += 1
            sub = col0 // 128
            s.op("pe", [lambda e, kc=kc: e.matmul(bank(b), lhsT=hxB[hb][:, kc, col0:col0 + 128], rhs=slot[:, kc, :], start=(kc == 0), stop=(kc == 15)) for kc in range(16)],
                 reads=[slotkey] + [("hx", hb, sub, kc) for kc in range(16)], writes=[("ps", b)])
            return b

        def qk_post(b, nh, gidx, use_rope, tglob, dst_ap, qi, vdst=None):
            q = qtmp[qi]
            K = lambda name: (name, qi)
            W = nh * 128
            pb = bank(b)[:, 0:W]
            ssq = stat[:, 8 + 4 * qi:8 + 4 * qi + nh]
            rq = stat[:, 32 + 4 * qi:32 + 4 * qi + nh]
            junk = q["junk"]
            s.op("act", [lambda e, h=h: e.activation(out=junk[:, h * 128:(h + 1) * 128], in_=pb[:, h * 128:(h + 1) * 128], func=AF.Square, accum_out=ssq[:, h:h + 1])
                         for h in range(nh)],
                 reads=[("ps", b)], writes=[K("t1"), K("t2"), K("ssq")])
            yield
            if vdst is not None:
                s.op("act", lambda e: e.activation(out=vdst, in_=bank(b)[:, 256:512], func=AF.Copy), reads=[("ps", b)], writes=[("V", qi)])
                yield
            s.op("act", lambda e: e.activation(out=ssq, in_=ssq, func=AF.Sqrt, scale=1.0 / 128, bias=EPS), reads=[K("ssq")], writes=[K("ssq")])
            yield
            s.op("dve", lambda e: e.reciprocal(out=rq, in_=ssq), reads=[K("ssq")], writes=[K("rq")])
            yield
            gname = "gq" if gidx == 0 else "gk"
            qr = q["qr"][:, 0:W]
            if use_rope:
                s.op("dve", [lambda e, h=h: e.scalar_tensor_tensor(out=q["qf"][:, h * 128:(h + 1) * 128], in0=pb[:, h * 128:(h + 1) * 128], scalar=rq[:, h:h + 1],
                                                                  in1=gqk[:, gidx, :], op0=ALU.mult, op1=ALU.mult) for h in range(nh)],
                     reads=[("ps", b), K("rq"), gname], writes=[K("qf")])
                yield "front_done"
                q4 = q["qf"][:, 0:W].rearrange("p (h i t) -> p h i t", h=nh, t=2)
                o4 = qr.rearrange("p (h i t) -> p h i t", h=nh, t=2)
                xa = q4[:, :, :, 0]
                xb = q4[:, :, :, 1]
                cosb = rope[:, 0, tglob, :].unsqueeze(1).to_broadcast([128, nh, 64])
                sinb = rope[:, 1, tglob, :].unsqueeze(1).to_broadcast([128, nh, 64])
                t1 = q["t1"][:, 0:nh, :]
                t2 = q["t2"][:, 0:nh, :]
                s.op("dve", lambda e: e.tensor_tensor(out=t1, in0=xa, in1=cosb, op=ALU.mult), reads=[K("qf"), "rope"], writes=[K("t1")])
                yield
                s.op("dve", lambda e: e.tensor_tensor(out=t2, in0=xb, in1=sinb, op=ALU.mult), reads=[K("qf"), "rope"], writes=[K("t2")])
                yield
                s.op("dve", lambda e: e.tensor_tensor(out=o4[:, :, :, 0], in0=t1, in1=t2, op=ALU.subtract), reads=[K("t1"), K("t2")], writes=[K("qr0")])
                yield
                s.op("dve", lambda e: e.tensor_tensor(out=t1, in0=xa, in1=sinb, op=ALU.mult), reads=[K("qf"), "rope"], writes=[K("t1")])
                yield
                s.op("dve", lambda e: e.tensor_tensor(out=t2, in0=xb, in1=cosb, op=ALU.mult), reads=[K("qf"), "rope"], writes=[K("t2")])
                yield
                s.op("dve", lambda e: e.tensor_tensor(out=o4[:, :, :, 1], in0=t1, in1=t2, op=ALU.add), reads=[K("t1"), K("t2")], writes=[K("qr1")])
                yield "rope_done"
            else:
                s.op("dve", [lambda e, h=h: e.scalar_tensor_tensor(out=qr[:, h * 128:(h + 1) * 128], in0=pb[:, h * 128:(h + 1) * 128], scalar=rq[:, h:h + 1],
                                                                  in1=gqk[:, gidx, :], op0=ALU.mult, op1=ALU.mult) for h in range(nh)],
                     reads=[("ps", b), K("rq"), gname], writes=[K("qr0"), K("qr1")])
                yield
            pbT = bank_bf(b, nh)
            s.op("pe", [lambda e, h=h: e.transpose(out=pbT[:, h, :], in_=qr[:, h * 128:(h + 1) * 128], identity=ident) for h in range(nh)],
                 reads=[K("qr0"), K("qr1"), "ident"], writes=[("ps", b)])
            yield
            s.op("act", lambda e: e.activation(out=dst_ap, in_=pbT, func=AF.Copy), reads=[("ps", b)], writes=[("qkT_out", qi)])
            yield

        def run_interleaved(gens, until=None):
            active = list(gens)
            while active:
                for g_ in list(active):
                    try:
                        r_ = next(g_)
                        if until is not None and r_ == until:
                            active.remove(g_)
                    except StopIteration:
                        active.remove(g_)

        stage_a_all([ctx_d[cs * 128:(cs + 1) * 128, :] for cs in range(2)], 1, 1, 1)
        stage_a_all([x_d[s4 * 128:(s4 + 1) * 128, :] for s4 in range(4)], 0, 0, 0)
        it = 8
        slot, skey = wring[it % 2], ("wr", it % 2)
        gl = []
        for cs in range(2):
            b = inproj_group(slot, skey, 1, cs * 128)
            gl.append(qk_post(b, 2, 1, False, 0, kT[:, :, cs * 128:(cs + 1) * 128], cs, vdst=V[:, cs, :]))
        ring1_load()
        it += 1
        pending = gl
        for m in range(4):
            hb = m % 2
            a2_todo = None
            for n in range(5):
                slot, skey = wring[it % 2], ("wr", it % 2)
                for pr_ in range(2):
                    gl = []
                    for dd in range(2):
                        s4 = pr_ * 2 + dd
                        tg = m * 4 + s4
                        b = inproj_group(slot, skey, hb, s4 * 128)
                        if n < 2:
                            s.op("act", lambda e, b=b, tg=tg, n=n: e.activation(out=U[:, tg, n * 512:(n + 1) * 512], in_=bank(b), func=AF.Copy),
                                 reads=[("ps", b)], writes=[("U", dd)])
                        elif n < 4:
                            h0 = (n - 2) * 4
                            gl.append(qk_post(b, 4, 0, True, tg, QT[:, h0:h0 + 4, tg * 128:(tg + 1) * 128], dd))
                        else:
                            gl.append(qk_post(b, 2, 1, True, tg, kT[:, :, CTX + tg * 128:CTX + (tg + 1) * 128], dd, vdst=V[:, 2 + tg, :]))
                    if pr_ == 1:
                        ring1_load()
                        it += 1
                    if n < 2:
                        run_interleaved(pending)
                        pending = []
                        if a2_todo is not None:
                            stage_a2(*a2_todo)
                            a2_todo = None
                        if m + 1 < 4:
                            sa = n * 2 + pr_
                            tgn = (m + 1) * 4 + sa
                            sl_ = stage_a1(x_d[tgn * 128:(tgn + 1) * 128, :])
                            a2_todo = (sl_, 1 - hb, sa * 128, 0, 0)
                    else:
                        if a2_todo is not None:
                            stage_a2(*a2_todo)
                            a2_todo = None
                        run_interleaved(gl, "front_done")
                        run_interleaved(pending)
                        run_interleaved(gl, "rope_done")
                        pending = gl
                if n == 4:
                    precast(1)
        run_interleaved(pending)

        s.barrier()
        if debug_stop == 1:
            s.dma("sp", dbg["qT"], QT, "g_dbg", reads=[])
            s.dma("sp", dbg["kT"], kT, "g_dbg")
            s.dma("sp", dbg["V"], V, "g_dbg")
            s.dma("sp", dbg["U"], U, "g_dbg")
            s.dma("sp", dbg["mod"], modsb, "g_dbg")
            s.final_wait("sp", ["g_dbg"])
            s.generate(nc)
            return nc

        s.dma("pool", poolw, pw_d, "g_c", writes=["poolw"])
        s.dma("sp", GB[:, 2, :], fn_d.partition_broadcast(128), "g_sm", writes=[("GB", 2)])
        ring2_items = list(range(8, 24))
        r2 = {"next": 0}

        def ring2_load():
            i = r2["next"]
            if i >= len(ring2_items):
                return
            s.dma("pool", wring2[i % 2], wada_d[ring2_items[i]], "g_r2_%d" % (i % 2), writes=[("wr2", i % 2)])
            r2["next"] += 1

        ring2_load()
        ring2_load()
        precast(1)

        pb_i = {"i": 0}

        def next_pbank():
            b = pb_i["i"] % 6
            pb_i["i"] += 1
            return b

        for g in range(4):
            w = POOL_WINDOWS[g]
            for i in range(4):
                psl = (g * 4 + i) % 2
                for kc in range(2):
                    b = next_pbank()
                    c0 = g * 256 + kc * 128
                    fns = []
                    for tb in range(4):
                        tblk = 4 * i + tb
                        outp = bank(b)[:, tb * 128:(tb + 1) * 128]
                        selfband = 3 if tblk == 0 else (4 if tblk == 15 else 0)
                        lst = [(tblk, selfband)]
                        if tblk > 0:
                            lst.append((tblk - 1, 1))
                        if tblk < 15:
                            lst.append((tblk + 1, 2))
                        for li, (sb_, bt) in enumerate(lst):
                            fns.append(lambda e, outp=outp, sb_=sb_, bt=bt, li=li, nl=len(lst), c0=c0, g=g:
                                       e.matmul(outp, lhsT=U[:, sb_, c0:c0 + 128], rhs=bands[:, g * 5 + bt, :], start=(li == 0), stop=(li == nl - 1)))
                    s.op("pe", fns, reads=["bands"], writes=[("ps", b)])
                    s.op("act", lambda e, b=b, psl=psl, kc=kc, w=w: e.activation(out=pooled[psl][:, kc, :], in_=bank(b), func=AF.Copy, scale=1.0 / w),
                         reads=[("ps", b)], writes=[("pooled", psl, kc)])
                    if i == 0:
                        s.op("dve", lambda e, b=b, psl=psl, kc=kc, g=g: e.tensor_tensor(out=pooled[psl][:, kc, 0:8], in0=bank(b)[:, 0:8], in1=pfix[:, g, 0, :], op=ALU.mult),
                             reads=[("ps", b), "pfix"], writes=[("pooled", psl, kc)])
                    if i == 3:
                        s.op("dve", lambda e, b=b, psl=psl, kc=kc, g=g: e.tensor_tensor(out=pooled[psl][:, kc, 504:512], in0=bank(b)[:, 504:512], in1=pfix[:, g, 1, :], op=ALU.mult),
                             reads=[("ps", b), "pfix"], writes=[("pooled", psl, kc)])
                for dc in range(2):
                    b = next_pbank()
                    s.op("pe", [lambda e, b=b, kc=kc, dc=dc, g=g, psl=psl: e.matmul(bank(b), lhsT=poolw[:, g, kc, dc * 128:(dc + 1) * 128], rhs=pooled[psl][:, kc, :],
                                                                                    start=(kc == 0), stop=(kc == 1)) for kc in range(2)],
                         reads=["poolw", ("pooled", psl, 0), ("pooled", psl, 1)], writes=[("ps", b)])
                    s.op("act", lambda e, b=b, g=g, dc=dc, i=i: e.activation(out=CP[:, g * 2 + dc, i * 512:(i + 1) * 512], in_=bank(b), func=AF.Copy,
                                                                             scale=vecs[:, 32 + g * 2 + dc:33 + g * 2 + dc]),
                         reads=[("ps", b), "vecs"], writes=[("CP", i)])
            precast(1)

        steps = []
        for g in range(2):
            for i in range(4):
                for hh in range(4):
                    for pr in range(9):
                        steps.append((g, i, 4 * g + hh, pr))
        OB, DB = 4, 5

        def emit_qk(si):
            g, i, h, pr = steps[si]
            pp = si % 2
            s.op("pe", [lambda e, c=c, g=g, h=h, i=i, pp=pp, pr=pr: e.matmul(P[pp][:, c * 512:(c + 1) * 512], lhsT=kT[:, g, (2 * pr + c) * 128:(2 * pr + c + 1) * 128],
                                                                                rhs=QT[:, h, i * 512:(i + 1) * 512], start=True, stop=True) for c in range(2)],
                 reads=[("q", h, i)], writes=[("ps", 2 * pp), ("ps", 2 * pp + 1)])
            es = si % 3
            s.op("act", lambda e, pp=pp, es=es: e.activation(out=E[es], in_=P[pp][:], func=AF.Exp), reads=[("ps", 2 * pp), ("ps", 2 * pp + 1)], writes=[("E", es)])

        def emit_pv(si):
            g, i, h, pr = steps[si]
            es = si % 3
            fns = []
            for c in range(2):
                cc_ = 2 * pr + c
                fns.append(lambda e, c=c, cc_=cc_, g=g, es=es: e.matmul(bank(OB), lhsT=V[:, cc_, g * 128:(g + 1) * 128], rhs=E[es][:, c * 512:(c + 1) * 512],
                                                                       start=(cc_ == 0), stop=(cc_ == 17)))
                fns.append(lambda e, c=c, cc_=cc_, es=es: e.matmul(bank(DB), lhsT=ones, rhs=E[es][:, c * 512:(c + 1) * 512], start=(cc_ == 0), stop=(cc_ == 17)))
            s.op("pe", fns, reads=[("E", es), "ones"], writes=[("ps", OB), ("ps", DB)])
            if pr == 8:
                s.op("act", lambda e: e.activation(out=Dsb, in_=bank(DB), func=AF.Copy), reads=[("ps", DB)], writes=["Dsb"])
                s.op("dve", lambda e: e.tensor_copy(out=Osb, in_=bank(OB)), reads=[("ps", OB)], writes=["Osb"])
                s.op("dve", lambda e: e.reciprocal(out=rD, in_=Dsb), reads=["Dsb"], writes=["rD"])
                s.op("dve", lambda e, h=h, i=i: e.tensor_tensor(out=QT[:, h, i * 512:(i + 1) * 512], in0=Osb, in1=rD, op=ALU.mult),
                     reads=["Osb", "rD"], writes=[("q", h, i)])

        wch = {"i": 0}

        def wada_rest_step():
            i = wch["i"]
            if i >= 16:
                return
            wada_chunk(8 + i, wring2[i % 2], ("wr2", i % 2))
            ring2_load()
            wch["i"] += 1

        emit_qk(0)
        for si in range(len(steps)):
            if si + 1 < len(steps):
                emit_qk(si + 1)
            emit_pv(si)
            if steps[si][3] == 8:
                itn = si // 9
                if itn % 2 == 1:
                    wada_rest_step()
                if itn % 4 == 3:
                    precast(1)
        while wch["i"] < 16:
            wada_rest_step()
        s.op("dve", lambda e: e.scalar_tensor_tensor(out=ABv[:, 2, :], in0=modsb[:, 48:64, 0], scalar=1.0, in1=vecs[:, 16:32], op0=ALU.add, op1=ALU.mult),
             reads=["modsb", "vecs"], writes=["AB2"])

        s.barrier()
        if debug_stop == 2:
            s.dma("sp", dbg["qT"], QT, "g_dbg")
            s.dma("sp", dbg["CP"], CP, "g_dbg")
            s.dma("sp", dbg["GB"], GB, "g_dbg")
            s.dma("sp", dbg["mod"], modsb, "g_dbg")
            s.final_wait("sp", ["g_dbg"])
            s.generate(nc)
            return nc

        items3 = []
        for i in range(4):
            items3 += [("wo", k, wo_s) for k in range(8)] + [("wgu", k, wgu_s) for k in range(NF)] + [("wd", k, wd_s) for k in range(24)]
        r3 = {"next": 0, "cons": 0}

        def ring3_load():
            i = r3["next"]
            if i >= len(items3):
                return
            kind, k, src = items3[i]
            s.dma("sp", ring[i % 3], src[k], "g_r3_%d" % (i % 3), reads=[("scr", kind, k // 4)], writes=[("r3", i % 3)])
            r3["next"] += 1

        def ring3_take():
            i = r3["cons"]
            r3["cons"] += 1
            return ring[i % 3], ("r3", i % 3)

        ssp = stat[:, 40:56].rearrange("p (a b) -> p a b", a=4)

        def evac_residual(bset, n, gidx, tcount):
            for sq_ in range(4):
                tb = tmp3[tcount["i"] % 2]
                tk = ("tmp3", tcount["i"] % 2)
                jb = sg[tcount["i"] % 2]
                jk = ("sg", tcount["i"] % 2)
                tcount["i"] += 1
                ncols = slice(n * 512, (n + 1) * 512)
                s.op("dve", lambda e, tb=tb, b=bset + sq_, ncols=ncols: e.tensor_tensor(out=tb, in0=bank(b), in1=GB[:, gidx, ncols], op=ALU.mult),
                     reads=[("ps", bset + sq_)], writes=[tk])
                s.op("dve", lambda e, tb=tb, sq_=sq_, ncols=ncols: e.tensor_tensor(out=x1[:, sq_, ncols], in0=tb, in1=x1[:, sq_, ncols], op=ALU.add),
                     reads=[tk, ("x1", sq_)], writes=[("x1", sq_)])
                s.op("act", lambda e, jb=jb, sq_=sq_, ncols=ncols, n=n: e.activation(out=jb, in_=x1[:, sq_, ncols], func=AF.Square, accum_out=ssp[:, sq_, n:n + 1]),
                     reads=[("x1", sq_)], writes=[jk, ("ssp", sq_, n)])

        def rstd_from_partials(sq_):
            ss = stat[:, 16 + sq_:17 + sq_]
            rs = stat[:, 20 + sq_:21 + sq_]
            s.op("dve", lambda e: e.tensor_reduce(out=ss, in_=ssp[:, sq_, :], axis=AX.X, op=ALU.add), reads=[("ssp", sq_, n_) for n_ in range(4)], writes=[("ss3", sq_)])
            s.op("act", lambda e: e.activation(out=ss, in_=ss, func=AF.Sqrt, scale=1.0 / D, bias=EPS), reads=[("ss3", sq_)], writes=[("ss3", sq_)])
            s.op("dve", lambda e: e.reciprocal(out=rs, in_=ss), reads=[("ss3", sq_)], writes=[("rs3", sq_)])
            return rs

        for _ in range(3):
            ring3_load()
        tcount = {"i": 0}
        for i in range(4):
            c0 = i * 512
            for q_ in range(4):
                s.dma("pool", x1[:, q_, :], x_d[c0 + q_ * 128:c0 + (q_ + 1) * 128, :], "g_x3%d" % (q_ % 2), writes=[("x1", q_)])
            if i == 0:
                precast(3)
            for n in range(4):
                bset = 0 if n % 2 == 0 else 4
                for kh in range(2):
                    slot, skey = ring3_take()
                    w3 = slot.rearrange("p (k c) -> p k c", k=8)
                    for sq_ in range(4):
                        s.op("pe", [lambda e, kc=kc, sq_=sq_, kh=kh, bset=bset, w3=w3, c0=c0: e.matmul(bank(bset + sq_), lhsT=cat_chunk(kh * 8 + kc)[:, c0 + sq_ * 128:c0 + (sq_ + 1) * 128],
                                                                                               rhs=w3[:, kc, :], start=(kh == 0 and kc == 0), stop=(kh == 1 and kc == 7))
                                    for kc in range(8)],
                             reads=[skey, ("cat", i)], writes=[("ps", bset + sq_)])
                    ring3_load()
                evac_residual(bset, n, 0, tcount)
            if i == 0:
                precast(3)
            rss = [rstd_from_partials(sq_) for sq_ in range(4)]

            def hf_a1(sq_):
                xb = sq_ % 2
                s.op("act", lambda e, sq_=sq_, xb=xb: e.activation(out=xn3[xb], in_=x1[:, sq_, :], func=AF.Copy, scale=rss[sq_]),
                     reads=[("x1", sq_), ("rs3", sq_)], writes=[("xn3", xb)])

            def hf_a2(sq_, c0=c0, i=i):
                xb = sq_ % 2
                pr = sq_ % 2
                trp = P[pr][:].bitcast(BF16).rearrange("p (j t) -> p j t", j=16)
                s.op("pe", [lambda e, j=j, trp=trp, xb=xb: e.transpose(out=trp[:, j, :], in_=xn3[xb][:, j * 128:(j + 1) * 128], identity=ident) for j in range(16)],
                     reads=[("xn3", xb), "ident"], writes=[("ps", 2 * pr), ("ps", 2 * pr + 1)])
                for j in range(16):
                    s.op("dve", lambda e, j=j, trp=trp, sq_=sq_, c0=c0: e.tensor_scalar(out=cat_chunk(j)[:, c0 + sq_ * 128:c0 + (sq_ + 1) * 128], in0=trp[:, j, :],
                                                                                        scalar1=ABv[:, 2, j:j + 1], scalar2=modsb[:, 32 + j, 0:1], op0=ALU.mult, op1=ALU.add),
                         reads=[("ps", 2 * pr + j // 8), "AB2", "modsb"], writes=[("hf", sq_, j)], wars=[("cat", i)])

            hf_a1(0)
            for sq_ in range(4):
                if sq_ + 1 < 4:
                    hf_a1(sq_ + 1)
                hf_a2(sq_)
            for f in range(NF):
                slot, skey = ring3_take()
                w4 = slot.rearrange("p (g k c) -> p g k c", g=2, k=16)
                gb_, ub_ = 2 * (f % 4), 2 * (f % 4) + 1
                s.op("pe", [lambda e, kc=kc, w4=w4, gb_=gb_, c0=c0: e.matmul(bank(gb_), lhsT=w4[:, 0, kc, :], rhs=cat_chunk(kc)[:, c0:c0 + 512], start=(kc == 0), stop=(kc == 15)) for kc in range(16)]
                     + [lambda e, kc=kc, w4=w4, ub_=ub_, c0=c0: e.matmul(bank(ub_), lhsT=w4[:, 1, kc, :], rhs=cat_chunk(kc)[:, c0:c0 + 512], start=(kc == 0), stop=(kc == 15)) for kc in range(16)],
                     reads=[skey] + [("hf", q_, kc) for q_ in range(4) for kc in range(16)], writes=[("ps", gb_), ("ps", ub_)])
                ring3_load()
                sgb = sg[f % 2]
                s.op("act", lambda e, sgb=sgb, gb_=gb_: e.activation(out=sgb, in_=bank(gb_), func=AF.Silu), reads=[("ps", gb_)], writes=[("sg", f % 2)])
                s.op("dve", lambda e, sgb=sgb, ub_=ub_, f=f: e.tensor_tensor(out=actT[:, f, :], in0=bank(ub_), in1=sgb, op=ALU.mult),
                     reads=[("ps", ub_), ("sg", f % 2)], writes=[("act", f)])
            for n in range(4):
                bset = 0 if n % 2 == 0 else 4
                for j in range(6):
                    slot, skey = ring3_take()
                    w3 = slot.rearrange("p (k c) -> p k c", k=8)
                    nfl = 8 if j < 5 else 4
                    for sq_ in range(4):
                        s.op("pe", [lambda e, fl=fl, j=j, sq_=sq_, bset=bset, w3=w3: e.matmul(bank(bset + sq_), lhsT=actT[:, 8 * j + fl, sq_ * 128:(sq_ + 1) * 128], rhs=w3[:, fl, :],
                                                                                             start=(j == 0 and fl == 0), stop=(8 * j + fl == NF - 1)) for fl in range(nfl)],
                             reads=[skey] + [("act", 8 * j + fl) for fl in range(nfl)], writes=[("ps", bset + sq_)])
                    ring3_load()
                evac_residual(bset, n, 1, tcount)
            for sq_ in range(4):
                rs = rstd_from_partials(sq_)
                s.op("dve", lambda e, sq_=sq_, rs=rs: e.scalar_tensor_tensor(out=x1[:, sq_, :], in0=x1[:, sq_, :], scalar=rs, in1=GB[:, 2, :], op0=ALU.mult, op1=ALU.mult),
                     reads=[("x1", sq_), ("rs3", sq_)], writes=[("x1", sq_)])
                s.dma("pool", out_d[c0 + sq_ * 128:c0 + (sq_ + 1) * 128, :], x1[:, sq_, :], "g_st%d" % (sq_ % 2), reads=[("x1", sq_)])

        s.final_wait("pool", ["g_st0", "g_st1"])
        s.generate(nc)
    return nc


def _pool_consts():
    cm = np.zeros((128, 22, 128), np.float32)
    cm[:, 0, :] = np.eye(128, dtype=np.float32)
    cm[:, 1, :] = 1.0
    pfix = np.zeros((128, 4, 2, 8), np.float32)
    sidx = np.arange(128)[:, None]
    tidx = np.arange(128)[None, :]
    for g, w in enumerate(POOL_WINDOWS):
        h = w // 2
        inwin = ((sidx >= tidx - h) & (sidx < tidx + h)).astype(np.float32)
        eye = np.eye(128, dtype=np.float32)
        self_ = inwin - w * eye
        prev = (sidx - 128 >= tidx - h).astype(np.float32)
        nxt = (sidx + 128 < tidx + h).astype(np.float32)
        cnt_first = np.minimum(np.arange(128) + h, T) - np.maximum(np.arange(128) - h, 0)
        first = inwin - np.diag(cnt_first.astype(np.float32))
        tl = np.arange(128) + (T - 128)
        cnt_last = np.minimum(tl + h, T) - np.maximum(tl - h, 0)
        last = inwin - np.diag(cnt_last.astype(np.float32))
        for k, mtx in enumerate((self_, prev, nxt, first, last)):
            cm[:, 2 + g * 5 + k, :] = mtx
        pfix[:, g, 0, :] = 1.0 / cnt_first[0:8]
        pfix[:, g, 1, :] = 1.0 / cnt_last[120:128]
    return cm, pfix


def _rope_tables():
    n_rows = T // 64
    rows = np.repeat(np.arange(n_rows, dtype=np.float32), 64)
    cols = np.tile(np.arange(64, dtype=np.float32), n_rows)
    freqs = (np.float32(10000.0) ** (-np.arange(0, 64, 2, dtype=np.float32) / np.float32(64))).astype(np.float32)
    ang = np.concatenate([rows[:, None] * freqs, cols[:, None] * freqs], axis=-1).astype(np.float32)
    cos = np.cos(ang).astype(np.float32).reshape(16, 128, 64).transpose(1, 0, 2)
    sin = np.sin(ang).astype(np.float32).reshape(16, 128, 64).transpose(1, 0, 2)
    return np.ascontiguousarray(np.stack([cos, sin], axis=1))


def _prep_shared(c_ctx, w_ada, b_ada, norm_mix, norm_ffn, w_in, pool_w, pool_scale, q_norm, k_norm, w_out, w_gate, w_up, w_down, final_norm):
    f = np.float32
    w_ada = np.asarray(w_ada, f)[0]
    wada = np.zeros((24, 128, 17, 512), f)
    wada[:, :, 0:16, :] = w_ada.reshape(16, 128, 24, 512).transpose(2, 1, 0, 3)
    wada[:, 0, 16, :] = np.asarray(b_ada, f)[0].reshape(24, 512)
    vecs = np.zeros((128, 40), f)
    vecs[:, 0:16] = np.asarray(norm_mix, f)[0].reshape(16, 128).T
    vecs[:, 16:32] = np.asarray(norm_ffn, f)[0].reshape(16, 128).T
    vecs[:, 32:40] = np.asarray(pool_scale, f)[0].reshape(8, 128).T
    win = np.ascontiguousarray(np.asarray(w_in, f)[0].reshape(16, 128, 5, 512).transpose(2, 1, 0, 3))
    pw = np.ascontiguousarray(np.asarray(pool_w, f)[0].reshape(4, 2, 128, 256).transpose(2, 0, 1, 3))
    wo = np.asarray(w_out, f)[0].reshape(2, 8, 128, 4, 512).transpose(3, 0, 2, 1, 4)
    wo = np.ascontiguousarray(wo).reshape(8, 128, 4096)
    wg = np.asarray(w_gate, f)[0].reshape(16, 128, NF, 128).transpose(2, 1, 0, 3)
    wu = np.asarray(w_up, f)[0].reshape(16, 128, NF, 128).transpose(2, 1, 0, 3)
    wgu = np.ascontiguousarray(np.stack([wg, wu], axis=2)).reshape(NF, 128, 4096)
    wdn = np.asarray(w_down, f)[0].reshape(NF, 128, 4, 512)
    wd = np.zeros((4, 6, 128, 8, 512), f)
    for j in range(6):
        nfl = 8 if j < 5 else 4
        wd[:, j, :, 0:nfl, :] = wdn[8 * j:8 * j + nfl].transpose(2, 1, 0, 3)
    wd = wd.reshape(24, 128, 4096)
    cm, pfix = _pool_consts()
    return dict(wada=wada, vecs=vecs, gq=np.ascontiguousarray(np.asarray(q_norm, f)[0]), gk=np.ascontiguousarray(np.asarray(k_norm, f)[0]),
                fn=np.ascontiguousarray(np.asarray(final_norm, f)), win=win, pw=pw, wo=wo, wgu=wgu, wd=wd, cmat=cm, rope=_rope_tables(), pfix=pfix)


def make_in_maps(x, c, ctx, c_ctx, **wts):
    shared = _prep_shared(c_ctx, **wts)
    x = np.asarray(x, np.float32)
    c = np.asarray(c, np.float32)
    ctx = np.asarray(ctx, np.float32)
    c_ctx = np.asarray(c_ctx, np.float32)
    maps = []
    for b in range(8):
        cc = np.ascontiguousarray(np.stack([c[b].reshape(16, 128).T, c_ctx.reshape(16, 128).T], axis=1))
        m = dict(shared)
        m["x"] = np.ascontiguousarray(x[b])
        m["ctx"] = np.ascontiguousarray(ctx[b])
        m["cc"] = cc
        maps.append(m)
    return maps


def kernel(x, c, ctx, c_ctx, w_ada, b_ada, norm_mix, norm_ffn, w_in, pool_w, pool_scale, q_norm, k_norm,
           w_out, w_gate, w_up, w_down, final_norm):
    in_maps = make_in_maps(x, c, ctx, c_ctx, w_ada=w_ada, b_ada=b_ada, norm_mix=norm_mix, norm_ffn=norm_ffn, w_in=w_in,
                           pool_w=pool_w, pool_scale=pool_scale, q_norm=q_norm, k_norm=k_norm, w_out=w_out,
                           w_gate=w_gate, w_up=w_up, w_down=w_down, final_norm=final_norm)
    nc = build_program(DEBUG_STOP)
    res = run_bass_kernel_spmd(nc, in_maps, core_ids=list(range(8)))
    if DEBUG_STOP is not None:
        return res
    return np.stack([np.asarray(r["out"], np.float32) for r in res.results], axis=0)
```

```python
import contextlib
import math

import numpy as np

import concourse.bass as bass
import concourse.mybir as mybir
from concourse.bass_utils import run_bass_kernel_spmd

F32 = mybir.dt.float32
BF16 = mybir.dt.bfloat16
AF = mybir.ActivationFunctionType
ALU = mybir.AluOpType
AX = mybir.AxisListType

T = 2048
D = 2048
NT = 16
CTX = 256
DFF = 5632
NF = 44
EPS = 1e-6
POOL_WINDOWS = (2, 4, 8, 16)
ENGS = ("pe", "act", "dve", "pool", "sp")

DEBUG_STOP = None


class Sched:
    def __init__(self):
        self.prog = {e: [] for e in ENGS}
        self.cnt = {}
        self.lastw = {}
        self.readers = {}
        self.seen = {e: {} for e in ENGS}
        self.bg_groups = set()

    def _collect(self, reads, writes):
        toks = {}

        def add(k, v):
            if toks.get(k, 0) < v:
                toks[k] = v

        for r in reads:
            t = self.lastw.get(r)
            if t is not None:
                add(*t)
        for w in writes:
            t = self.lastw.get(w)
            if t is not None:
                add(*t)
            for k, v in self.readers.get(w, {}).items():
                add(k, v)
        return toks

    def _emit_waits(self, eng, toks, skip_self):
        for k in sorted(toks):
            v = toks[k]
            if k == eng and skip_self:
                continue
            if self.seen[eng].get(k, 0) >= v:
                continue
            self.seen[eng][k] = v
            self.prog[eng].append(("wait", k, v))

    def _record(self, tok, reads, writes):
        k, v = tok
        for r in reads:
            d = self.readers.setdefault(r, {})
            if d.get(k, 0) < v:
                d[k] = v
        for w in writes:
            self.lastw[w] = tok
            self.readers[w] = {}

    def op(self, eng, fns, reads=(), writes=(), wars=()):
        if callable(fns):
            fns = [fns]
        toks = self._collect(reads, tuple(writes) + tuple(wars))
        self._emit_waits(eng, toks, skip_self=(eng == "pe"))
        self.cnt[eng] = self.cnt.get(eng, 0) + 1
        tok = (eng, self.cnt[eng])
        self.prog[eng].append(("op", fns, eng, 1))
        self._record(tok, reads, writes)
        return tok

    def dma(self, queue, out, in_, group, reads=(), writes=(), background=False):
        if group not in self.cnt:
            self.cnt[group] = 0
            if background:
                self.bg_groups.add(group)
        toks = self._collect(reads, writes)
        if self.cnt[group] > 0 and toks.get(group, 0) < self.cnt[group]:
            toks[group] = self.cnt[group]
        self._emit_waits(queue, toks, skip_self=False)
        self.cnt[group] += 16
        tok = (group, self.cnt[group])
        self.prog[queue].append(("op", [lambda e, o=out, i=in_: e.dma_start(out=o, in_=i)], group, 16))
        self._record(tok, reads, writes)
        return tok

    def barrier(self):
        toks = {k: v for k, v in self.cnt.items() if v > 0 and k not in self.bg_groups}
        for e in ENGS:
            self._emit_waits(e, dict(toks), skip_self=(e == "pe"))
        self.lastw = {r: t for r, t in self.lastw.items() if t[0] in self.bg_groups}
        self.readers = {}

    def final_wait(self, eng, groups):
        toks = {g: self.cnt[g] for g in groups if self.cnt.get(g, 0) > 0}
        self._emit_waits(eng, toks, skip_self=False)

    def generate(self, nc):
        keys = [k for k, v in self.cnt.items() if v > 0]
        with contextlib.ExitStack() as st:
            sems = {k: st.enter_context(nc.semaphore("s_" + str(k))) for k in keys}
            block = st.enter_context(nc.Block())

            def run(engname):
                def body(e):
                    for item in self.prog[engname]:
                        if item[0] == "wait":
                            e.wait_ge(sems[item[1]], item[2])
                        else:
                            _, fns, semk, inc = item
                            ins = None
                            for f in fns:
                                ins = f(e)
                            ins.then_inc(sems[semk], inc)
                return body

            block.tensor(run("pe"))
            block.scalar(run("act"))
            block.vector(run("dve"))
            block.gpsimd(run("pool"))
            block.sync(run("sp"))


def build_program(debug_stop=None):
    nc = bass.Bass("TRN2", target_bir_lowering=False)

    def din(name, shape):
        return nc.dram_tensor(name, list(shape), F32, kind="ExternalInput").ap()

    x_d = din("x", [T, D])
    ctx_d = din("ctx", [CTX, D])
    cc_d = din("cc", [128, 2, 16])
    wada_d = din("wada", [24, 128, 17, 512])
    vecs_d = din("vecs", [128, 40])
    gq_d = din("gq", [128])
    gk_d = din("gk", [128])
    fn_d = din("fn", [D])
    win_d = din("win", [5, 128, 16, 512])
    pw_d = din("pw", [128, 4, 2, 256])
    wo_d = din("wo", [8, 128, 4096])
    wgu_d = din("wgu", [NF, 128, 4096])
    wd_d = din("wd", [24, 128, 4096])
    cmat_d = din("cmat", [128, 22, 128])
    rope_d = din("rope", [128, 2, 16, 64])
    pfix_d = din("pfix", [128, 4, 2, 8])
    out_d = nc.dram_tensor("out", [T, D], F32, kind="ExternalOutput").ap()
    wo_s = nc.dram_tensor("wo_s", [8, 128, 4096], BF16, kind="Internal").ap()
    wgu_s = nc.dram_tensor("wgu_s", [NF, 128, 4096], BF16, kind="Internal").ap()
    wd_s = nc.dram_tensor("wd_s", [24, 128, 4096], BF16, kind="Internal").ap()
    dbg = {}
    if debug_stop is not None:
        dbg["qT"] = nc.dram_tensor("dbg_qT", [128, 8, T], BF16, kind="ExternalOutput").ap()
        dbg["kT"] = nc.dram_tensor("dbg_kT", [128, 2, T + CTX], BF16, kind="ExternalOutput").ap()
        dbg["V"] = nc.dram_tensor("dbg_V", [128, 18, 256], BF16, kind="ExternalOutput").ap()
        dbg["U"] = nc.dram_tensor("dbg_U", [128, 16, 1024], BF16, kind="ExternalOutput").ap()
        dbg["CP"] = nc.dram_tensor("dbg_CP", [128, 8, T], BF16, kind="ExternalOutput").ap()
        dbg["GB"] = nc.dram_tensor("dbg_GB", [128, 3, D], F32, kind="ExternalOutput").ap()
        dbg["mod"] = nc.dram_tensor("dbg_mod", [128, 64, 2], F32, kind="ExternalOutput").ap()

    s = Sched()
    with contextlib.ExitStack() as st:
        ARENA_W = 53200
        arena = st.enter_context(nc.sbuf_tensor("arena", [128, ARENA_W], F32))
        P = [st.enter_context(nc.psum_tensor("ps%d" % i, [128, 1024], F32)) for i in range(4)]

        def view(off, shape, dt):
            esz = 4 if dt == F32 else 2
            n = 1
            for d_ in shape[1:]:
                n *= d_
            nb = n * esz
            assert off % 4 == 0 and nb % 4 == 0 and off + nb <= ARENA_W * 4, (off, shape)
            a = arena[:, off // 4:(off + nb) // 4]
            if dt != F32:
                a = a.bitcast(dt)
            if len(shape) == 3:
                a = a.rearrange("p (a b) -> p a b", a=shape[1])
            elif len(shape) == 4:
                a = a.rearrange("p (a b c) -> p a b c", a=shape[1], b=shape[2])
            return a

        def bank(b):
            return P[b // 2][:, (b % 2) * 512:(b % 2) * 512 + 512]

        def bank_bf(b, n):
            return bank(b).bitcast(BF16)[:, 0:n * 128].rearrange("p (a b) -> p a b", a=n)

        O_QT = 0
        O_CP = 32768
        O_GB = 65536
        O_SM = 90112
        O_OV = 93696
        QT = view(O_QT, [128, 8, T], BF16)
        CP = view(O_CP, [128, 8, T], BF16)
        GB = view(O_GB, [128, 3, D], F32)
        o = O_SM
        ident = view(o, [128, 128], BF16); o += 256
        ones = view(o, [128, 128], BF16); o += 256
        vecs = view(o, [128, 40], F32); o += 160
        ABv = view(o, [128, 3, 16], F32); o += 192
        modsb = view(o, [128, 64, 2], F32); o += 512
        gqk = view(o, [128, 2, 128], F32); o += 1024
        sc2 = view(o, [128, 16, 2], BF16); o += 64
        cc = view(o, [128, 2, 16], F32); o += 128
        stat = view(o, [128, 64], F32); o += 256
        pfix = view(o, [128, 4, 2, 8], F32); o += 256
        assert o <= O_OV

        def cat_chunk(j):
            return CP[:, j, :] if j < 8 else QT[:, j - 8, :]

        o = O_OV
        U = view(o, [128, 16, 1024], BF16); o += 32768
        kT = view(o, [128, 2, T + CTX], BF16); o += 9216
        V = view(o, [128, 18, 256], BF16); o += 9216
        bands = view(o, [128, 20, 128], BF16); o += 5120
        scbc = view(o, [128, 16, 128], BF16); o += 4096
        O_OV2 = o
        wring = [view(o + i * 17408, [128, 17, 512], BF16) for i in range(2)]; o += 34816
        xt = [view(o + i * 8192, [128, D], F32) for i in range(2)]; o += 16384
        assert o <= ARENA_W * 4
        hxB = [view(O_CP + i * 16384, [128, 16, 512], BF16) for i in range(2)]
        o = O_GB
        xn = [view(o + i * 4096, [128, D], BF16) for i in range(2)]; o += 8192
        QTMP = 5120

        def mk_qtmp(b0):
            return dict(qr=view(b0, [128, 512], BF16), t1=view(b0 + 1024, [128, 4, 64], F32), junk=view(b0 + 1024, [128, 512], F32),
                        t2=view(b0 + 2048, [128, 4, 64], F32), qf=view(b0 + 3072, [128, 512], F32))

        qtmp = [mk_qtmp(o), mk_qtmp(O_OV2 + 34816 + 16384)]
        assert O_OV2 + 34816 + 16384 + QTMP <= ARENA_W * 4
        o += QTMP
        rope = view(o, [128, 2, 16, 64], F32); o += 8192
        assert o <= O_SM
        o = O_OV2
        E = [view(o + i * 2048, [128, 1024], BF16) for i in range(3)]; o += 6144
        rD = view(o, [128, 512], F32); o += 2048
        pooled = [view(o + i * 2048, [128, 2, 512], BF16) for i in range(2)]; o += 4096
        poolw = view(o, [128, 4, 2, 256], BF16); o += 4096
        wring2 = [view(o + i * 17408, [128, 17, 512], BF16) for i in range(2)]; o += 34816
        Osb = view(o, [128, 512], F32); o += 2048
        Dsb = view(o, [128, 512], F32); o += 2048
        assert o <= ARENA_W * 4
        o = O_OV
        x1 = view(o, [128, 4, D], F32); o += 32768
        actT = view(o, [128, NF, 512], BF16); o += 45056
        ring = [view(o + i * 8192, [128, 4096], BF16) for i in range(3)]; o += 24576
        xn3 = [view(o + i * 4096, [128, D], BF16) for i in range(2)]; o += 8192
        sg = [view(o + i * 2048, [128, 512], F32) for i in range(2)]; o += 4096
        tmp3 = [view(o + i * 2048, [128, 512], F32) for i in range(2)]; o += 4096
        assert o <= ARENA_W * 4, o

        s.dma("sp", cc, cc_d, "g_sm", writes=["cc"])
        s.dma("sp", vecs, vecs_d, "g_sm", writes=["vecs"])
        s.dma("sp", gqk[:, 0, :], gq_d.partition_broadcast(128), "g_sm", writes=["gq"])
        s.dma("sp", gqk[:, 1, :], gk_d.partition_broadcast(128), "g_sm", writes=["gk"])
        s.dma("sp", rope, rope_d, "g_sm", writes=["rope"])
        s.dma("sp", pfix, pfix_d, "g_sm", writes=["pfix"])

        s.op("act", lambda e: e.activation(out=sc2[:, :, 0], in_=cc[:, 0, :], func=AF.Silu), reads=["cc"], writes=["sc2a"])
        s.op("act", lambda e: e.activation(out=sc2[:, :, 1], in_=cc[:, 1, :], func=AF.Silu), reads=["cc"], writes=["sc2b"])
        s.op("dve", lambda e: e.tensor_copy(out=scbc, in_=sc2[:, :, 0:1].to_broadcast([128, 16, 128])), reads=["sc2a"], writes=["scbc"])
        s.op("dve", lambda e: e.tensor_scalar(out=gqk[:, 0, :], in0=gqk[:, 0, :], scalar1=1.0 / math.sqrt(128.0), scalar2=0.0, op0=ALU.mult, op1=ALU.add),
             reads=["gq"], writes=["gq"])

        pc_list = []
        for kind, src, dst, n in (("wo", wo_d, wo_s, 8), ("wgu", wgu_d, wgu_s, NF), ("wd", wd_d, wd_s, 24)):
            for c in range(n // 4):
                pc_list.append((kind, c, src, dst))
        pc_state = {"i": 0}

        def precast(k=1):
            for _ in range(k):
                i = pc_state["i"]
                if i >= len(pc_list):
                    return
                kind, c, src, dst = pc_list[i]
                s.dma("pool", dst[4 * c:4 * c + 4].rearrange("a p f -> (a p) f"), src[4 * c:4 * c + 4].rearrange("a p f -> (a p) f"),
                      "g_pc%d" % (i % 3), writes=[("scr", kind, c)], background=True)
                pc_state["i"] += 1

        MODBANK = 6
        GBANK = 7
        modps = bank(MODBANK)[:, 0:128].rearrange("p (a b) -> p a b", b=2)

        def wada_chunk(ch, slot, slotkey):
            part = ch // 4
            if part in (2, 5):
                gi = 0 if part == 2 else 1
                cols = slice((ch % 4) * 512, (ch % 4) * 512 + 512)
                fns = [lambda e, kc=kc: e.matmul(bank(GBANK), lhsT=scbc[:, kc, :], rhs=slot[:, kc, :], start=(kc == 0), stop=False) for kc in range(16)]
                fns.append(lambda e: e.matmul(bank(GBANK), lhsT=ones[0:1, :], rhs=slot[0:1, 16, :], start=False, stop=True))
                s.op("pe", fns, reads=[slotkey, "scbc", "ones"], writes=[("ps", GBANK)])
                s.op("act", lambda e: e.activation(out=GB[:, gi, cols], in_=bank(GBANK), func=AF.Copy), reads=[("ps", GBANK)], writes=[("GB", gi, ch % 4)])
            else:
                base = {0: 0, 1: 16, 3: 32, 4: 48}[part] + (ch % 4) * 4
                fns = []
                for blk in range(4):
                    dst = modps[:, base + blk, :]
                    for kc in range(16):
                        fns.append(lambda e, kc=kc, blk=blk, dst=dst: e.matmul(dst, lhsT=slot[:, kc, blk * 128:(blk + 1) * 128], rhs=sc2[:, kc, :],
                                                                               start=(kc == 0), stop=False))
                    fns.append(lambda e, blk=blk, dst=dst: e.matmul(dst, lhsT=slot[0:1, 16, blk * 128:(blk + 1) * 128], rhs=ones[0:1, 0:2], start=False, stop=True))
                s.op("pe", fns, reads=[slotkey, "sc2a", "sc2b", "ones"], writes=[("ps", MODBANK)])
                s.op("dve", lambda e: e.tensor_copy(out=modsb[:, base:base + 4, :], in_=modps[:, base:base + 4, :]), reads=[("ps", MODBANK)], writes=["modsb"])

        ring1_items = [("wada", ch) for ch in range(8)] + [("win", 4)] + [("win", n) for _ in range(4) for n in range(5)]
        r1 = {"next": 0}

        def ring1_load():
            i = r1["next"]
            if i >= len(ring1_items):
                return
            kind, idx = ring1_items[i]
            slot = wring[i % 2]
            if kind == "wada":
                s.dma("pool", slot, wada_d[idx], "g_r1_%d" % (i % 2), writes=[("wr", i % 2)])
            else:
                s.dma("pool", slot[:, 0:16, :], win_d[idx], "g_r1_%d" % (i % 2), writes=[("wr", i % 2)])
            r1["next"] += 1

        ring1_load()
        ring1_load()
        s.dma("pool", ones, cmat_d[:, 1, :], "g_c", writes=["ones"])
        s.dma("pool", ident, cmat_d[:, 0, :], "g_c", writes=["ident"])
        s.dma("pool", bands, cmat_d[:, 2:22, :], "g_c", writes=["bands"])
        for ch in range(8):
            wada_chunk(ch, wring[ch % 2], ("wr", ch % 2))
            ring1_load()
        s.op("dve", lambda e: e.scalar_tensor_tensor(out=ABv[:, 0, :], in0=modsb[:, 16:32, 0], scalar=1.0, in1=vecs[:, 0:16], op0=ALU.add, op1=ALU.mult),
             reads=["modsb", "vecs"], writes=["AB0"])
        s.op("dve", lambda e: e.scalar_tensor_tensor(out=ABv[:, 1, :], in0=modsb[:, 16:32, 1], scalar=1.0, in1=vecs[:, 0:16], op0=ALU.add, op1=ALU.mult),
             reads=["modsb", "vecs"], writes=["AB1"])

        trp_pairs = [0, 1]
        cnt = {"sub": 0, "grp": 0}

        def stage_a1(src_ap):
            k = cnt["sub"]
            cnt["sub"] += 1
            sl = k % 2
            ss = stat[:, sl:sl + 1]
            rs = stat[:, 2 + sl:3 + sl]
            s.dma("sp", xt[sl], src_ap, "g_x%d" % sl, writes=[("xt", sl)])
            s.op("act", lambda e: e.activation(out=xn[sl], in_=xt[sl], func=AF.Square, accum_out=ss), reads=[("xt", sl)], writes=[("xn", sl), ("ss", sl)])
            s.op("act", lambda e: e.activation(out=ss, in_=ss, func=AF.Sqrt, scale=1.0 / D, bias=EPS), reads=[("ss", sl)], writes=[("ss", sl)])
            s.op("dve", lambda e: e.reciprocal(out=rs, in_=ss), reads=[("ss", sl)], writes=[("rs", sl)])
            s.op("act", lambda e: e.activation(out=xn[sl], in_=xt[sl], func=AF.Copy, scale=rs), reads=[("xt", sl), ("rs", sl)], writes=[("xn", sl)])
            return sl

        def stage_a2(sl, hb, col0, a_idx, b_col):
            sub = col0 // 128
            trp = P[0][:].bitcast(BF16).rearrange("p (j t) -> p j t", j=16)
            s.op("pe", [lambda e, j=j: e.transpose(out=trp[:, j, :], in_=xn[sl][:, j * 128:(j + 1) * 128], identity=ident) for j in range(16)],
                 reads=[("xn", sl), "ident"], writes=[("ps", 0), ("ps", 1)])
            for j in range(16):
                s.op("dve", lambda e, j=j: e.tensor_scalar(out=hxB[hb][:, j, col0:col0 + 128], in0=trp[:, j, :], scalar1=ABv[:, a_idx, j:j + 1],
                                                           scalar2=modsb[:, j, b_col:b_col + 1], op0=ALU.mult, op1=ALU.add),
                     reads=[("ps", j // 8), "AB%d" % a_idx, "modsb"], writes=[("hx", hb, sub, j)])

        def stage_a_all(srcs, hb, a_idx, b_col):
            sl_next = stage_a1(srcs[0])
            for k in range(len(srcs)):
                sl_cur = sl_next
                if k + 1 < len(srcs):
                    sl_next = stage_a1(srcs[k + 1])
                stage_a2(sl_cur, hb, k * 128, a_idx, b_col)

        def inproj_group(slot, slotkey, hb, col0):
            b = 2 + cnt["grp"] % 6
            cnt["grp"] += 1
            sub = col0 // 128
            s.op("pe", [lambda e, kc=kc: e.matmul(bank(b), lhsT=hxB[hb][:, kc, col0:col0 + 128], rhs=slot[:, kc, :], start=(kc == 0), stop=(kc == 15)) for kc in range(16)],
                 reads=[slotkey] + [("hx", hb, sub, kc) for kc in range(16)], writes=[("ps", b)])
            return b

        def qk_post(b, nh, gidx, use_rope, tglob, dst_ap, qi, vdst=None):
            q = qtmp[qi]
            K = lambda name: (name, qi)
            W = nh * 128
            pb = bank(b)[:, 0:W]
            ssq = stat[:, 8 + 4 * qi:8 + 4 * qi + nh]
            rq = stat[:, 32 + 4 * qi:32 + 4 * qi + nh]
            junk = q["junk"]
            s.op("act", [lambda e, h=h: e.activation(out=junk[:, h * 128:(h + 1) * 128], in_=pb[:, h * 128:(h + 1) * 128], func=AF.Square, accum_out=ssq[:, h:h + 1])
                         for h in range(nh)],
                 reads=[("ps", b)], writes=[K("t1"), K("t2"), K("ssq")])
            yield
            if vdst is not None:
                s.op("act", lambda e: e.activation(out=vdst, in_=bank(b)[:, 256:512], func=AF.Copy), reads=[("ps", b)], writes=[("V", qi)])
                yield
            s.op("act", lambda e: e.activation(out=ssq, in_=ssq, func=AF.Sqrt, scale=1.0 / 128, bias=EPS), reads=[K("ssq")], writes=[K("ssq")])
            yield
            s.op("dve", lambda e: e.reciprocal(out=rq, in_=ssq), reads=[K("ssq")], writes=[K("rq")])
            yield
            gname = "gq" if gidx == 0 else "gk"
            qr = q["qr"][:, 0:W]
            if use_rope:
                s.op("dve", [lambda e, h=h: e.scalar_tensor_tensor(out=q["qf"][:, h * 128:(h + 1) * 128], in0=pb[:, h * 128:(h + 1) * 128], scalar=rq[:, h:h + 1],
                                                                  in1=gqk[:, gidx, :], op0=ALU.mult, op1=ALU.mult) for h in range(nh)],
                     reads=[("ps", b), K("rq"), gname], writes=[K("qf")])
                yield "front_done"
                q4 = q["qf"][:, 0:W].rearrange("p (h i t) -> p h i t", h=nh, t=2)
                o4 = qr.rearrange("p (h i t) -> p h i t", h=nh, t=2)
                xa = q4[:, :, :, 0]
                xb = q4[:, :, :, 1]
                cosb = rope[:, 0, tglob, :].unsqueeze(1).to_broadcast([128, nh, 64])
                sinb = rope[:, 1, tglob, :].unsqueeze(1).to_broadcast([128, nh, 64])
                t1 = q["t1"][:, 0:nh, :]
                t2 = q["t2"][:, 0:nh, :]
                s.op("dve", lambda e: e.tensor_tensor(out=t1, in0=xa, in1=cosb, op=ALU.mult), reads=[K("qf"), "rope"], writes=[K("t1")])
                yield
                s.op("dve", lambda e: e.tensor_tensor(out=t2, in0=xb, in1=sinb, op=ALU.mult), reads=[K("qf"), "rope"], writes=[K("t2")])
                yield
                s.op("dve", lambda e: e.tensor_tensor(out=o4[:, :, :, 0], in0=t1, in1=t2, op=ALU.subtract), reads=[K("t1"), K("t2")], writes=[K("qr0")])
                yield
                s.op("dve", lambda e: e.tensor_tensor(out=t1, in0=xa, in1=sinb, op=ALU.mult), reads=[K("qf"), "rope"], writes=[K("t1")])
                yield
                s.op("dve", lambda e: e.tensor_tensor(out=t2, in0=xb, in1=cosb, op=ALU.mult), reads=[K("qf"), "rope"], writes=[K("t2")])
                yield
                s.op("dve", lambda e: e.tensor_tensor(out=o4[:, :, :, 1], in0=t1, in1=t2, op=ALU.add), reads=[K("t1"), K("t2")], writes=[K("qr1")])
                yield "rope_done"
            else:
                s.op("dve", [lambda e, h=h: e.scalar_tensor_tensor(out=qr[:, h * 128:(h + 1) * 128], in0=pb[:, h * 128:(h + 1) * 128], scalar=rq[:, h:h + 1],
                                                                  in1=gqk[:, gidx, :], op0=ALU.mult, op1=ALU.mult) for h in range(nh)],
                     reads=[("ps", b), K("rq"), gname], writes=[K("qr0"), K("qr1")])
                yield
            pbT = bank_bf(b, nh)
            s.op("pe", [lambda e, h=h: e.transpose(out=pbT[:, h, :], in_=qr[:, h * 128:(h + 1) * 128], identity=ident) for h in range(nh)],
                 reads=[K("qr0"), K("qr1"), "ident"], writes=[("ps", b)])
            yield
            s.op("act", lambda e: e.activation(out=dst_ap, in_=pbT, func=AF.Copy), reads=[("ps", b)], writes=[("qkT_out", qi)])
            yield

        def run_interleaved(gens, until=None):
            active = list(gens)
            while active:
                for g_ in list(active):
                    try:
                        r_ = next(g_)
                        if until is not None and r_ == until:
                            active.remove(g_)
                    except StopIteration:
                        active.remove(g_)

        stage_a_all([ctx_d[cs * 128:(cs + 1) * 128, :] for cs in range(2)], 1, 1, 1)
        stage_a_all([x_d[s4 * 128:(s4 + 1) * 128, :] for s4 in range(4)], 0, 0, 0)
        it = 8
        slot, skey = wring[it % 2], ("wr", it % 2)
        gl = []
        for cs in range(2):
            b = inproj_group(slot, skey, 1, cs * 128)
            gl.append(qk_post(b, 2, 1, False, 0, kT[:, :, cs * 128:(cs + 1) * 128], cs, vdst=V[:, cs, :]))
        ring1_load()
        it += 1
        pending = gl
        for m in range(4):
            hb = m % 2
            a2_todo = None
            for n in range(5):
                slot, skey = wring[it % 2], ("wr", it % 2)
                for pr_ in range(2):
                    gl = []
                    for dd in range(2):
                        s4 = pr_ * 2 + dd
                        tg = m * 4 + s4
                        b = inproj_group(slot, skey, hb, s4 * 128)
                        if n < 2:
                            s.op("act", lambda e, b=b, tg=tg, n=n: e.activation(out=U[:, tg, n * 512:(n + 1) * 512], in_=bank(b), func=AF.Copy),
                                 reads=[("ps", b)], writes=[("U", dd)])
                        elif n < 4:
                            h0 = (n - 2) * 4
                            gl.append(qk_post(b, 4, 0, True, tg, QT[:, h0:h0 + 4, tg * 128:(tg + 1) * 128], dd))
                        else:
                            gl.append(qk_post(b, 2, 1, True, tg, kT[:, :, CTX + tg * 128:CTX + (tg + 1) * 128], dd, vdst=V[:, 2 + tg, :]))
                    if pr_ == 1:
                        ring1_load()
                        it += 1
                    if n < 2:
                        run_interleaved(pending)
                        pending = []
                        if a2_todo is not None:
                            stage_a2(*a2_todo)
                            a2_todo = None
                        if m + 1 < 4:
                            sa = n * 2 + pr_
                            tgn = (m + 1) * 4 + sa
                            sl_ = stage_a1(x_d[tgn * 128:(tgn + 1) * 128, :])
                            a2_todo = (sl_, 1 - hb, sa * 128, 0, 0)
                    else:
                        if a2_todo is not None:
                            stage_a2(*a2_todo)
                            a2_todo = None
                        run_interleaved(gl, "front_done")
                        run_interleaved(pending)
                        run_interleaved(gl, "rope_done")
                        pending = gl
                if n == 4:
                    precast(1)
        run_interleaved(pending)

        s.barrier()
        if debug_stop == 1:
            s.dma("sp", dbg["qT"], QT, "g_dbg", reads=[])
            s.dma("sp", dbg["kT"], kT, "g_dbg")
            s.dma("sp", dbg["V"], V, "g_dbg")
            s.dma("sp", dbg["U"], U, "g_dbg")
            s.dma("sp", dbg["mod"], modsb, "g_dbg")
            s.final_wait("sp", ["g_dbg"])
            s.generate(nc)
            return nc

        s.dma("pool", poolw, pw_d, "g_c", writes=["poolw"])
        s.dma("sp", GB[:, 2, :], fn_d.partition_broadcast(128), "g_sm", writes=[("GB", 2)])
        ring2_items = list(range(8, 24))
        r2 = {"next": 0}

        def ring2_load():
            i = r2["next"]
            if i >= len(ring2_items):
                return
            s.dma("pool", wring2[i % 2], wada_d[ring2_items[i]], "g_r2_%d" % (i % 2), writes=[("wr2", i % 2)])
            r2["next"] += 1

        ring2_load()
        ring2_load()
        precast(1)

        pb_i = {"i": 0}

        def next_pbank():
            b = pb_i["i"] % 6
            pb_i["i"] += 1
            return b

        for g in range(4):
            w = POOL_WINDOWS[g]
            for i in range(4):
                psl = (g * 4 + i) % 2
                for kc in range(2):
                    b = next_pbank()
                    c0 = g * 256 + kc * 128
                    fns = []
                    for tb in range(4):
                        tblk = 4 * i + tb
                        outp = bank(b)[:, tb * 128:(tb + 1) * 128]
                        selfband = 3 if tblk == 0 else (4 if tblk == 15 else 0)
                        lst = [(tblk, selfband)]
                        if tblk > 0:
                            lst.append((tblk - 1, 1))
                        if tblk < 15:
                            lst.append((tblk + 1, 2))
                        for li, (sb_, bt) in enumerate(lst):
                            fns.append(lambda e, outp=outp, sb_=sb_, bt=bt, li=li, nl=len(lst), c0=c0, g=g:
                                       e.matmul(outp, lhsT=U[:, sb_, c0:c0 + 128], rhs=bands[:, g * 5 + bt, :], start=(li == 0), stop=(li == nl - 1)))
                    s.op("pe", fns, reads=["bands"], writes=[("ps", b)])
                    s.op("act", lambda e, b=b, psl=psl, kc=kc, w=w: e.activation(out=pooled[psl][:, kc, :], in_=bank(b), func=AF.Copy, scale=1.0 / w),
                         reads=[("ps", b)], writes=[("pooled", psl, kc)])
                    if i == 0:
                        s.op("dve", lambda e, b=b, psl=psl, kc=kc, g=g: e.tensor_tensor(out=pooled[psl][:, kc, 0:8], in0=bank(b)[:, 0:8], in1=pfix[:, g, 0, :], op=ALU.mult),
                             reads=[("ps", b), "pfix"], writes=[("pooled", psl, kc)])
                    if i == 3:
                        s.op("dve", lambda e, b=b, psl=psl, kc=kc, g=g: e.tensor_tensor(out=pooled[psl][:, kc, 504:512], in0=bank(b)[:, 504:512], in1=pfix[:, g, 1, :], op=ALU.mult),
                             reads=[("ps", b), "pfix"], writes=[("pooled", psl, kc)])
                for dc in range(2):
                    b = next_pbank()
                    s.op("pe", [lambda e, b=b, kc=kc, dc=dc, g=g, psl=psl: e.matmul(bank(b), lhsT=poolw[:, g, kc, dc * 128:(dc + 1) * 128], rhs=pooled[psl][:, kc, :],
                                                                                    start=(kc == 0), stop=(kc == 1)) for kc in range(2)],
                         reads=["poolw", ("pooled", psl, 0), ("pooled", psl, 1)], writes=[("ps", b)])
                    s.op("act", lambda e, b=b, g=g, dc=dc, i=i: e.activation(out=CP[:, g * 2 + dc, i * 512:(i + 1) * 512], in_=bank(b), func=AF.Copy,
                                                                             scale=vecs[:, 32 + g * 2 + dc:33 + g * 2 + dc]),
                         reads=[("ps", b), "vecs"], writes=[("CP", i)])
            precast(1)

        steps = []
        for g in range(2):
            for i in range(4):
                for hh in range(4):
                    for pr in range(9):
                        steps.append((g, i, 4 * g + hh, pr))
        OB, DB = 4, 5

        def emit_qk(si):
            g, i, h, pr = steps[si]
            pp = si % 2
            s.op("pe", [lambda e, c=c, g=g, h=h, i=i, pp=pp, pr=pr: e.matmul(P[pp][:, c * 512:(c + 1) * 512], lhsT=kT[:, g, (2 * pr + c) * 128:(2 * pr + c + 1) * 128],
                                                                                rhs=QT[:, h, i * 512:(i + 1) * 512], start=True, stop=True) for c in range(2)],
                 reads=[("q", h, i)], writes=[("ps", 2 * pp), ("ps", 2 * pp + 1)])
            es = si % 3
            s.op("act", lambda e, pp=pp, es=es: e.activation(out=E[es], in_=P[pp][:], func=AF.Exp), reads=[("ps", 2 * pp), ("ps", 2 * pp + 1)], writes=[("E", es)])

        def emit_pv(si):
            g, i, h, pr = steps[si]
            es = si % 3
            fns = []
            for c in range(2):
                cc_ = 2 * pr + c
                fns.append(lambda e, c=c, cc_=cc_, g=g, es=es: e.matmul(bank(OB), lhsT=V[:, cc_, g * 128:(g + 1) * 128], rhs=E[es][:, c * 512:(c + 1) * 512],
                                                                       start=(cc_ == 0), stop=(cc_ == 17)))
                fns.append(lambda e, c=c, cc_=cc_, es=es: e.matmul(bank(DB), lhsT=ones, rhs=E[es][:, c * 512:(c + 1) * 512], start=(cc_ == 0), stop=(cc_ == 17)))
            s.op("pe", fns, reads=[("E", es), "ones"], writes=[("ps", OB), ("ps", DB)])
            if pr == 8:
                s.op("act", lambda e: e.activation(out=Dsb, in_=bank(DB), func=AF.Copy), reads=[("ps", DB)], writes=["Dsb"])
                s.op("dve", lambda e: e.tensor_copy(out=Osb, in_=bank(OB)), reads=[("ps", OB)], writes=["Osb"])
                s.op("dve", lambda e: e.reciprocal(out=rD, in_=Dsb), reads=["Dsb"], writes=["rD"])
                s.op("dve", lambda e, h=h, i=i: e.tensor_tensor(out=QT[:, h, i * 512:(i + 1) * 512], in0=Osb, in1=rD, op=ALU.mult),
                     reads=["Osb", "rD"], writes=[("q", h, i)])

        wch = {"i": 0}

        def wada_rest_step():
            i = wch["i"]
            if i >= 16:
                return
            wada_chunk(8 + i, wring2[i % 2], ("wr2", i % 2))
            ring2_load()
            wch["i"] += 1

        emit_qk(0)
        for si in range(len(steps)):
            if si + 1 < len(steps):
                emit_qk(si + 1)
            emit_pv(si)
            if steps[si][3] == 8:
                itn = si // 9
                if itn % 2 == 1:
                    wada_rest_step()
                if itn % 4 == 3:
                    precast(1)
        while wch["i"] < 16:
            wada_rest_step()
        s.op("dve", lambda e: e.scalar_tensor_tensor(out=ABv[:, 2, :], in0=modsb[:, 48:64, 0], scalar=1.0, in1=vecs[:, 16:32], op0=ALU.add, op1=ALU.mult),
             reads=["modsb", "vecs"], writes=["AB2"])

        s.barrier()
        if debug_stop == 2:
            s.dma("sp", dbg["qT"], QT, "g_dbg")
            s.dma("sp", dbg["CP"], CP, "g_dbg")
            s.dma("sp", dbg["GB"], GB, "g_dbg")
            s.dma("sp", dbg["mod"], modsb, "g_dbg")
            s.final_wait("sp", ["g_dbg"])
            s.generate(nc)
            return nc

        items3 = []
        for i in range(4):
            items3 += [("wo", k, wo_s) for k in range(8)] + [("wgu", k, wgu_s) for k in range(NF)] + [("wd", k, wd_s) for k in range(24)]
        r3 = {"next": 0, "cons": 0}

        def ring3_load():
            i = r3["next"]
            if i >= len(items3):
                return
            kind, k, src = items3[i]
            s.dma("sp", ring[i % 3], src[k], "g_r3_%d" % (i % 3), reads=[("scr", kind, k // 4)], writes=[("r3", i % 3)])
            r3["next"] += 1

        def ring3_take():
            i = r3["cons"]
            r3["cons"] += 1
            return ring[i % 3], ("r3", i % 3)

        ssp = stat[:, 40:56].rearrange("p (a b) -> p a b", a=4)

        def evac_residual(bset, n, gidx, tcount):
            for sq_ in range(4):
                tb = tmp3[tcount["i"] % 2]
                tk = ("tmp3", tcount["i"] % 2)
                jb = sg[tcount["i"] % 2]
                jk = ("sg", tcount["i"] % 2)
                tcount["i"] += 1
                ncols = slice(n * 512, (n + 1) * 512)
                s.op("dve", lambda e, tb=tb, b=bset + sq_, ncols=ncols: e.tensor_tensor(out=tb, in0=bank(b), in1=GB[:, gidx, ncols], op=ALU.mult),
                     reads=[("ps", bset + sq_)], writes=[tk])
                s.op("dve", lambda e, tb=tb, sq_=sq_, ncols=ncols: e.tensor_tensor(out=x1[:, sq_, ncols], in0=tb, in1=x1[:, sq_, ncols], op=ALU.add),
                     reads=[tk, ("x1", sq_)], writes=[("x1", sq_)])
                s.op("act", lambda e, jb=jb, sq_=sq_, ncols=ncols, n=n: e.activation(out=jb, in_=x1[:, sq_, ncols], func=AF.Square, accum_out=ssp[:, sq_, n:n + 1]),
                     reads=[("x1", sq_)], writes=[jk, ("ssp", sq_, n)])

        def rstd_from_partials(sq_):
            ss = stat[:, 16 + sq_:17 + sq_]
            rs = stat[:, 20 + sq_:21 + sq_]
            s.op("dve", lambda e: e.tensor_reduce(out=ss, in_=ssp[:, sq_, :], axis=AX.X, op=ALU.add), reads=[("ssp", sq_, n_) for n_ in range(4)], writes=[("ss3", sq_)])
            s.op("act", lambda e: e.activation(out=ss, in_=ss, func=AF.Sqrt, scale=1.0 / D, bias=EPS), reads=[("ss3", sq_)], writes=[("ss3", sq_)])
            s.op("dve", lambda e: e.reciprocal(out=rs, in_=ss), reads=[("ss3", sq_)], writes=[("rs3", sq_)])
            return rs

        for _ in range(3):
            ring3_load()
        tcount = {"i": 0}
        for i in range(4):
            c0 = i * 512
            for q_ in range(4):
                s.dma("pool", x1[:, q_, :], x_d[c0 + q_ * 128:c0 + (q_ + 1) * 128, :], "g_x3%d" % (q_ % 2), writes=[("x1", q_)])
            if i == 0:
                precast(3)
            for n in range(4):
                bset = 0 if n % 2 == 0 else 4
                for kh in range(2):
                    slot, skey = ring3_take()
                    w3 = slot.rearrange("p (k c) -> p k c", k=8)
                    for sq_ in range(4):
                        s.op("pe", [lambda e, kc=kc, sq_=sq_, kh=kh, bset=bset, w3=w3, c0=c0: e.matmul(bank(bset + sq_), lhsT=cat_chunk(kh * 8 + kc)[:, c0 + sq_ * 128:c0 + (sq_ + 1) * 128],
                                                                                               rhs=w3[:, kc, :], start=(kh == 0 and kc == 0), stop=(kh == 1 and kc == 7))
                                    for kc in range(8)],
                             reads=[skey, ("cat", i)], writes=[("ps", bset + sq_)])
                    ring3_load()
                evac_residual(bset, n, 0, tcount)
            if i == 0:
                precast(3)
            rss = [rstd_from_partials(sq_) for sq_ in range(4)]

            def hf_a1(sq_):
                xb = sq_ % 2
                s.op("act", lambda e, sq_=sq_, xb=xb: e.activation(out=xn3[xb], in_=x1[:, sq_, :], func=AF.Copy, scale=rss[sq_]),
                     reads=[("x1", sq_), ("rs3", sq_)], writes=[("xn3", xb)])

            def hf_a2(sq_, c0=c0, i=i):
                xb = sq_ % 2
                pr = sq_ % 2
                trp = P[pr][:].bitcast(BF16).rearrange("p (j t) -> p j t", j=16)
                s.op("pe", [lambda e, j=j, trp=trp, xb=xb: e.transpose(out=trp[:, j, :], in_=xn3[xb][:, j * 128:(j + 1) * 128], identity=ident) for j in range(16)],
                     reads=[("xn3", xb), "ident"], writes=[("ps", 2 * pr), ("ps", 2 * pr + 1)])
                for j in range(16):
                    s.op("dve", lambda e, j=j, trp=trp, sq_=sq_, c0=c0: e.tensor_scalar(out=cat_chunk(j)[:, c0 + sq_ * 128:c0 + (sq_ + 1) * 128], in0=trp[:, j, :],
                                                                                        scalar1=ABv[:, 2, j:j + 1], scalar2=modsb[:, 32 + j, 0:1], op0=ALU.mult, op1=ALU.add),
                         reads=[("ps", 2 * pr + j // 8), "AB2", "modsb"], writes=[("hf", sq_, j)], wars=[("cat", i)])

            hf_a1(0)
            for sq_ in range(4):
                if sq_ + 1 < 4:
                    hf_a1(sq_ + 1)
                hf_a2(sq_)
            for f in range(NF):
                slot, skey = ring3_take()
                w4 = slot.rearrange("p (g k c) -> p g k c", g=2, k=16)
                gb_, ub_ = 2 * (f % 4), 2 * (f % 4) + 1
                s.op("pe", [lambda e, kc=kc, w4=w4, gb_=gb_, c0=c0: e.matmul(bank(gb_), lhsT=w4[:, 0, kc, :], rhs=cat_chunk(kc)[:, c0:c0 + 512], start=(kc == 0), stop=(kc == 15)) for kc in range(16)]
                     + [lambda e, kc=kc, w4=w4, ub_=ub_, c0=c0: e.matmul(bank(ub_), lhsT=w4[:, 1, kc, :], rhs=cat_chunk(kc)[:, c0:c0 + 512], start=(kc == 0), stop=(kc == 15)) for kc in range(16)],
                     reads=[skey] + [("hf", q_, kc) for q_ in range(4) for kc in range(16)], writes=[("ps", gb_), ("ps", ub_)])
                ring3_load()
                sgb = sg[f % 2]
                s.op("act", lambda e, sgb=sgb, gb_=gb_: e.activation(out=sgb, in_=bank(gb_), func=AF.Silu), reads=[("ps", gb_)], writes=[("sg", f % 2)])
                s.op("dve", lambda e, sgb=sgb, ub_=ub_, f=f: e.tensor_tensor(out=actT[:, f, :], in0=bank(ub_), in1=sgb, op=ALU.mult),
                     reads=[("ps", ub_), ("sg", f % 2)], writes=[("act", f)])
            for n in range(4):
                bset = 0 if n % 2 == 0 else 4
                for j in range(6):
                    slot, skey = ring3_take()
                    w3 = slot.rearrange("p (k c) -> p k c", k=8)
                    nfl = 8 if j < 5 else 4
                    for sq_ in range(4):
                        s.op("pe", [lambda e, fl=fl, j=j, sq_=sq_, bset=bset, w3=w3: e.matmul(bank(bset + sq_), lhsT=actT[:, 8 * j + fl, sq_ * 128:(sq_ + 1) * 128], rhs=w3[:, fl, :],
                                                                                             start=(j == 0 and fl == 0), stop=(8 * j + fl == NF - 1)) for fl in range(nfl)],
                             reads=[skey] + [("act", 8 * j + fl) for fl in range(nfl)], writes=[("ps", bset + sq_)])
                    ring3_load()
                evac_residual(bset, n, 1, tcount)
            for sq_ in range(4):
                rs = rstd_from_partials(sq_)
                s.op("dve", lambda e, sq_=sq_, rs=rs: e.scalar_tensor_tensor(out=x1[:, sq_, :], in0=x1[:, sq_, :], scalar=rs, in1=GB[:, 2, :], op0=ALU.mult, op1=ALU.mult),
                     reads=[("x1", sq_), ("rs3", sq_)], writes=[("x1", sq_)])
                s.dma("pool", out_d[c0 + sq_ * 128:c0 + (sq_ + 1) * 128, :], x1[:, sq_, :], "g_st%d" % (sq_ % 2), reads=[("x1", sq_)])

        s.final_wait("pool", ["g_st0", "g_st1"])
        s.generate(nc)
    return nc


def _pool_consts():
    cm = np.zeros((128, 22, 128), np.float32)
    cm[:, 0, :] = np.eye(128, dtype=np.float32)
    cm[:, 1, :] = 1.0
    pfix = np.zeros((128, 4, 2, 8), np.float32)
    sidx = np.arange(128)[:, None]
    tidx = np.arange(128)[None, :]
    for g, w in enumerate(POOL_WINDOWS):
        h = w // 2
        inwin = ((sidx >= tidx - h) & (sidx < tidx + h)).astype(np.float32)
        eye = np.eye(128, dtype=np.float32)
        self_ = inwin - w * eye
        prev = (sidx - 128 >= tidx - h).astype(np.float32)
        nxt = (sidx + 128 < tidx + h).astype(np.float32)
        cnt_first = np.minimum(np.arange(128) + h, T) - np.maximum(np.arange(128) - h, 0)
        first = inwin - np.diag(cnt_first.astype(np.float32))
        tl = np.arange(128) + (T - 128)
        cnt_last = np.minimum(tl + h, T) - np.maximum(tl - h, 0)
        last = inwin - np.diag(cnt_last.astype(np.float32))
        for k, mtx in enumerate((self_, prev, nxt, first, last)):
            cm[:, 2 + g * 5 + k, :] = mtx
        pfix[:, g, 0, :] = 1.0 / cnt_first[0:8]
        pfix[:, g, 1, :] = 1.0 / cnt_last[120:128]
    return cm, pfix


def _rope_tables():
    n_rows = T // 64
    rows = np.repeat(np.arange(n_rows, dtype=np.float32), 64)
    cols = np.tile(np.arange(64, dtype=np.float32), n_rows)
    freqs = (np.float32(10000.0) ** (-np.arange(0, 64, 2, dtype=np.float32) / np.float32(64))).astype(np.float32)
    ang = np.concatenate([rows[:, None] * freqs, cols[:, None] * freqs], axis=-1).astype(np.float32)
    cos = np.cos(ang).astype(np.float32).reshape(16, 128, 64).transpose(1, 0, 2)
    sin = np.sin(ang).astype(np.float32).reshape(16, 128, 64).transpose(1, 0, 2)
    return np.ascontiguousarray(np.stack([cos, sin], axis=1))


def _prep_shared(c_ctx, w_ada, b_ada, norm_mix, norm_ffn, w_in, pool_w, pool_scale, q_norm, k_norm, w_out, w_gate, w_up, w_down, final_norm):
    f = np.float32
    w_ada = np.asarray(w_ada, f)[0]
    wada = np.zeros((24, 128, 17, 512), f)
    wada[:, :, 0:16, :] = w_ada.reshape(16, 128, 24, 512).transpose(2, 1, 0, 3)
    wada[:, 0, 16, :] = np.asarray(b_ada, f)[0].reshape(24, 512)
    vecs = np.zeros((128, 40), f)
    vecs[:, 0:16] = np.asarray(norm_mix, f)[0].reshape(16, 128).T
    vecs[:, 16:32] = np.asarray(norm_ffn, f)[0].reshape(16, 128).T
    vecs[:, 32:40] = np.asarray(pool_scale, f)[0].reshape(8, 128).T
    win = np.ascontiguousarray(np.asarray(w_in, f)[0].reshape(16, 128, 5, 512).transpose(2, 1, 0, 3))
    pw = np.ascontiguousarray(np.asarray(pool_w, f)[0].reshape(4, 2, 128, 256).transpose(2, 0, 1, 3))
    wo = np.asarray(w_out, f)[0].reshape(2, 8, 128, 4, 512).transpose(3, 0, 2, 1, 4)
    wo = np.ascontiguousarray(wo).reshape(8, 128, 4096)
    wg = np.asarray(w_gate, f)[0].reshape(16, 128, NF, 128).transpose(2, 1, 0, 3)
    wu = np.asarray(w_up, f)[0].reshape(16, 128, NF, 128).transpose(2, 1, 0, 3)
    wgu = np.ascontiguousarray(np.stack([wg, wu], axis=2)).reshape(NF, 128, 4096)
    wdn = np.asarray(w_down, f)[0].reshape(NF, 128, 4, 512)
    wd = np.zeros((4, 6, 128, 8, 512), f)
    for j in range(6):
        nfl = 8 if j < 5 else 4
        wd[:, j, :, 0:nfl, :] = wdn[8 * j:8 * j + nfl].transpose(2, 1, 0, 3)
    wd = wd.reshape(24, 128, 4096)
    cm, pfix = _pool_consts()
    return dict(wada=wada, vecs=vecs, gq=np.ascontiguousarray(np.asarray(q_norm, f)[0]), gk=np.ascontiguousarray(np.asarray(k_norm, f)[0]),
                fn=np.ascontiguousarray(np.asarray(final_norm, f)), win=win, pw=pw, wo=wo, wgu=wgu, wd=wd, cmat=cm, rope=_rope_tables(), pfix=pfix)


def make_in_maps(x, c, ctx, c_ctx, **wts):
    shared = _prep_shared(c_ctx, **wts)
    x = np.asarray(x, np.float32)
    c = np.asarray(c, np.float32)
    ctx = np.asarray(ctx, np.float32)
    c_ctx = np.asarray(c_ctx, np.float32)
    maps = []
    for b in range(8):
        cc = np.ascontiguousarray(np.stack([c[b].reshape(16, 128).T, c_ctx.reshape(16, 128).T], axis=1))
        m = dict(shared)
        m["x"] = np.ascontiguousarray(x[b])
        m["ctx"] = np.ascontiguousarray(ctx[b])
        m["cc"] = cc
        maps.append(m)
    return maps


def kernel(x, c, ctx, c_ctx, w_ada, b_ada, norm_mix, norm_ffn, w_in, pool_w, pool_scale, q_norm, k_norm,
           w_out, w_gate, w_up, w_down, final_norm):
    in_maps = make_in_maps(x, c, ctx, c_ctx, w_ada=w_ada, b_ada=b_ada, norm_mix=norm_mix, norm_ffn=norm_ffn, w_in=w_in,
                           pool_w=pool_w, pool_scale=pool_scale, q_norm=q_norm, k_norm=k_norm, w_out=w_out,
                           w_gate=w_gate, w_up=w_up, w_down=w_down, final_norm=final_norm)
    nc = build_program(DEBUG_STOP)
    res = run_bass_kernel_spmd(nc, in_maps, core_ids=list(range(8)))
    if DEBUG_STOP is not None:
        return res
    return np.stack([np.asarray(r["out"], np.float32) for r in res.results], axis=0)
```
